# Optimizing a Trainium2 kernel written in Bass

```python
import math
import jax, jax.numpy as jnp
from jax import lax
import numpy as np

D_MODEL = 1024
BATCH = 8
SEQ = 2048
DEPTH = 4
DEC_BATCH = 128
DEC_SEQ = 1
PAST_LEN = 16384
PAGE_SIZE = 128

N_EVEN = (DEPTH + 1) // 2
N_ODD = DEPTH // 2
HGRN_WIDTH = D_MODEL // 2
HGRN_EXPAND = 128
HGRN_HEADS = HGRN_WIDTH // HGRN_EXPAND
HGRN_DK = HGRN_EXPAND
HGRN_DV = HGRN_WIDTH // HGRN_HEADS
HGRN_CHUNK = 64
CONF_WIDTH = D_MODEL - HGRN_WIDTH
CONF_KERNEL = 31
SC_WIDTH = D_MODEL
SC_KERNEL = 3
D_FF = 256 * math.ceil(8 * D_MODEL / 3 / 256)
EVEN_IN = 4 * HGRN_WIDTH + 2 * CONF_WIDTH
ALPHA = (2 * DEPTH) ** 0.25
BETA = (8 * DEPTH) ** -0.25
LN_EPS = 1e-5
RMS_EPS = 1e-6

kernel_name = "hgrn2_conformer_shortconv_hybrid_step"


def _layernorm(x, w, b):
    xf = x.astype(jnp.float32)
    mu = jnp.mean(xf, axis=-1, keepdims=True)
    var = jnp.mean(jnp.square(xf - mu), axis=-1, keepdims=True)
    return ((xf - mu) * lax.rsqrt(var + LN_EPS) * w.astype(jnp.float32) + b.astype(jnp.float32)).astype(x.dtype)


def _causal_dwconv(u_ext, w):
    c = u_ext.shape[-1]
    return lax.conv_general_dilated(u_ext, w.astype(u_ext.dtype)[:, None, :], window_strides=(1,), padding='VALID',
                                    dimension_numbers=('NWC', 'WIO', 'NWC'), feature_group_count=c)


def _hgrn2_recurrence(q, k, v, g, s0):
    bsz, t_len, n_h = q.shape[:3]
    L = math.gcd(t_len, HGRN_CHUNK)
    n = t_len // L

    def to_chunks(a):
        return a.reshape(bsz, n, L, n_h, a.shape[-1]).transpose(1, 0, 3, 2, 4)

    tri = jnp.tril(jnp.ones((L, L), dtype=bool))

    def step(S, inp):
        qc, kc, vc, gc = inp
        b = jnp.cumsum(gc, axis=2)
        o = jnp.einsum('bhtk,bhkv->bhtv', qc * jnp.exp(b), S)
        diff = b[:, :, :, None, :] - b[:, :, None, :, :]
        decay = jnp.exp(jnp.where(tri[:, :, None], diff, -jnp.inf))
        scores = jnp.einsum('bhtk,bhsk,bhtsk->bhts', qc, kc, decay)
        o = o + jnp.einsum('bhts,bhsv->bhtv', scores, vc)
        b_last = b[:, :, -1:, :]
        S = jnp.exp(b_last[:, :, 0, :, None]) * S + jnp.einsum('bhsk,bhsv->bhkv', kc * jnp.exp(b_last - b), vc)
        return S, o

    s_final, o = lax.scan(step, s0.astype(jnp.float32), (to_chunks(q), to_chunks(k), to_chunks(v), to_chunks(g)))
    o = o.transpose(1, 0, 3, 2, 4).reshape(bsz, t_len, n_h, v.shape[-1])
    return o, s_final


def _even_mixer(x, s0, conf_buf, w_in, w_out, lb, gnorm_w, dw_w, dw_b, cln_w, cln_b):
    bsz, t_len, _ = x.shape
    f32 = jnp.float32
    p = x @ w_in
    hw = HGRN_WIDTH
    zq = p[..., 0:hw]
    zf = p[..., hw:2 * hw]
    vi = p[..., 2 * hw:3 * hw]
    zg = p[..., 3 * hw:4 * hw]
    ca = p[..., 4 * hw:4 * hw + CONF_WIDTH]
    cg = p[..., 4 * hw + CONF_WIDTH:]
    zf32 = zf.astype(f32)
    lb = lb.astype(f32)
    logf = jnp.logaddexp(jnp.log(lb), jnp.log1p(-lb) + jax.nn.log_sigmoid(zf32))
    kk = (1.0 - lb) * jax.nn.sigmoid(-zf32)
    qq = jax.nn.silu(zq.astype(f32)) * (HGRN_DK ** -0.5)
    shp_k = (bsz, t_len, HGRN_HEADS, HGRN_DK)
    o, s_new = _hgrn2_recurrence(qq.reshape(shp_k), kk.reshape(shp_k),
                                 vi.astype(f32).reshape(bsz, t_len, HGRN_HEADS, HGRN_DV),
                                 logf.reshape(shp_k), s0)
    o = o * lax.rsqrt(jnp.mean(jnp.square(o), axis=-1, keepdims=True) + RMS_EPS) * gnorm_w.astype(f32)
    o = o.reshape(bsz, t_len, HGRN_WIDTH) * jax.nn.silu(zg.astype(f32))
    u = ca * jax.nn.sigmoid(cg)
    u_ext = jnp.concatenate([conf_buf.astype(u.dtype), u], axis=1)
    c = _causal_dwconv(u_ext, dw_w) + dw_b
    c = jax.nn.silu(_layernorm(c, cln_w, cln_b))
    y = jnp.concatenate([o.astype(x.dtype), c.astype(x.dtype)], axis=-1) @ w_out
    return y, s_new.astype(s0.dtype), u_ext[:, -(CONF_KERNEL - 1):].astype(conf_buf.dtype)


def _odd_mixer(x, sc_buf, w_in, conv_w, w_out):
    p = x @ w_in
    bg = p[..., :SC_WIDTH]
    cg = p[..., SC_WIDTH:2 * SC_WIDTH]
    xv = p[..., 2 * SC_WIDTH:]
    z = cg * xv
    z_ext = jnp.concatenate([sc_buf.astype(z.dtype), z], axis=1)
    y = bg * _causal_dwconv(z_ext, conv_w)
    return y @ w_out, z_ext[:, -(SC_KERNEL - 1):].astype(sc_buf.dtype)


def _swiglu(x, w1, w3, w2):
    return (jax.nn.silu(x @ w1) * (x @ w3)) @ w2


def _trunk(x, s_hgrn, s_conf, s_sconv, lbs, w_in_even, w_out_even, hgrn_gnorm_w, conf_dw_w, conf_dw_b,
           conf_ln_w, conf_ln_b, sc_w_in, sc_conv_w, sc_w_out, ffn_w1, ffn_w3, ffn_w2,
           ln_mix_w, ln_mix_b, ln_ffn_w, ln_ffn_b):
    new_h, new_c, new_s = [], [], []
    for l in range(DEPTH):
        if l % 2 == 0:
            e = l // 2
            m, sh, cb = _even_mixer(x, s_hgrn[e], s_conf[e], w_in_even[e], w_out_even[e], lbs[e],
                                    hgrn_gnorm_w[e], conf_dw_w[e], conf_dw_b[e], conf_ln_w[e], conf_ln_b[e])
            new_h.append(sh)
            new_c.append(cb)
        else:
            o = l // 2
            m, sb = _odd_mixer(x, s_sconv[o], sc_w_in[o], sc_conv_w[o], sc_w_out[o])
            new_s.append(sb)
        x = _layernorm(ALPHA * x + m, ln_mix_w[l], ln_mix_b[l])
        x = _layernorm(ALPHA * x + _swiglu(x, ffn_w1[l], ffn_w3[l], ffn_w2[l]), ln_ffn_w[l], ln_ffn_b[l])
    return x, jnp.stack(new_h), jnp.stack(new_c), jnp.stack(new_s)


def setup_inputs(seed: int = 0) -> dict:
    key = jax.random.key(seed)
    ks = jax.random.split(key, 32)
    nrm = lambda k, shp, s: jax.random.normal(k, shp, jnp.float32) * s
    d = D_MODEL
    even_col_scale = jnp.concatenate([
        jnp.ones((2 * HGRN_WIDTH,)), jnp.full((HGRN_WIDTH,), BETA), jnp.ones((HGRN_WIDTH,)),
        jnp.full((CONF_WIDTH,), BETA), jnp.ones((CONF_WIDTH,))]).astype(jnp.float32)
    sc_col_scale = jnp.concatenate([jnp.ones((2 * SC_WIDTH,)), jnp.full((SC_WIDTH,), BETA)]).astype(jnp.float32)
    return {
        "x_prompt": nrm(ks[0], (BATCH, SEQ, d), 1.0),
        "x_sample": nrm(ks[1], (DEC_BATCH, DEC_SEQ, d), 1.0),
        "state_hgrn": nrm(ks[2], (N_EVEN, DEC_BATCH, HGRN_HEADS, HGRN_DK, HGRN_DV), 0.3),
        "state_conf": nrm(ks[3], (N_EVEN, DEC_BATCH, CONF_KERNEL - 1, CONF_WIDTH), 0.5),
        "state_sconv": nrm(ks[4], (N_ODD, DEC_BATCH, SC_KERNEL - 1, SC_WIDTH), 0.5),
        "w_in_even": nrm(ks[5], (N_EVEN, d, EVEN_IN), d ** -0.5) * even_col_scale,
        "w_out_even": nrm(ks[6], (N_EVEN, HGRN_WIDTH + CONF_WIDTH, d), (HGRN_WIDTH + CONF_WIDTH) ** -0.5 * BETA),
        "hgrn_lb_logits": nrm(ks[7], (N_EVEN, HGRN_WIDTH), 0.1),
        "hgrn_gnorm_w": 1.0 + nrm(ks[8], (N_EVEN, HGRN_DV), 0.05),
        "conf_dw_w": nrm(ks[9], (N_EVEN, CONF_KERNEL, CONF_WIDTH), CONF_KERNEL ** -0.5),
        "conf_dw_b": nrm(ks[10], (N_EVEN, CONF_WIDTH), 0.02),
        "conf_ln_w": 1.0 + nrm(ks[11], (N_EVEN, CONF_WIDTH), 0.05),
        "conf_ln_b": nrm(ks[12], (N_EVEN, CONF_WIDTH), 0.02),
        "sc_w_in": nrm(ks[13], (N_ODD, d, 3 * SC_WIDTH), d ** -0.5) * sc_col_scale,
        "sc_conv_w": nrm(ks[14], (N_ODD, SC_KERNEL, SC_WIDTH), SC_KERNEL ** -0.5),
        "sc_w_out": nrm(ks[15], (N_ODD, SC_WIDTH, d), SC_WIDTH ** -0.5 * BETA),
        "ffn_w1": nrm(ks[16], (DEPTH, d, D_FF), d ** -0.5),
        "ffn_w3": nrm(ks[17], (DEPTH, d, D_FF), d ** -0.5 * BETA),
        "ffn_w2": nrm(ks[18], (DEPTH, D_FF, d), D_FF ** -0.5 * BETA),
        "ln_mix_w": 1.0 + nrm(ks[19], (DEPTH, d), 0.05),
        "ln_mix_b": nrm(ks[20], (DEPTH, d), 0.02),
        "ln_ffn_w": 1.0 + nrm(ks[21], (DEPTH, d), 0.05),
        "ln_ffn_b": nrm(ks[22], (DEPTH, d), 0.02),
    }


def reference(x_prompt, x_sample, state_hgrn, state_conf, state_sconv, w_in_even, w_out_even, hgrn_lb_logits,
              hgrn_gnorm_w, conf_dw_w, conf_dw_b, conf_ln_w, conf_ln_b, sc_w_in, sc_conv_w, sc_w_out,
              ffn_w1, ffn_w3, ffn_w2, ln_mix_w, ln_mix_b, ln_ffn_w, ln_ffn_b):
    lbs = jnp.cumsum(jax.nn.softmax(hgrn_lb_logits.astype(jnp.float32), axis=0), axis=0)
    lbs = lbs - lbs[0:1]
    weights = (w_in_even, w_out_even, hgrn_gnorm_w, conf_dw_w, conf_dw_b, conf_ln_w, conf_ln_b,
               sc_w_in, sc_conv_w, sc_w_out, ffn_w1, ffn_w3, ffn_w2, ln_mix_w, ln_mix_b, ln_ffn_w, ln_ffn_b)
    zh = jnp.zeros((N_EVEN, BATCH, HGRN_HEADS, HGRN_DK, HGRN_DV), state_hgrn.dtype)
    zc = jnp.zeros((N_EVEN, BATCH, CONF_KERNEL - 1, CONF_WIDTH), state_conf.dtype)
    zs = jnp.zeros((N_ODD, BATCH, SC_KERNEL - 1, SC_WIDTH), state_sconv.dtype)
    y_prompt, h_p, c_p, s_p = _trunk(x_prompt, zh, zc, zs, lbs, *weights)
    y_sample, h_s, c_s, s_s = _trunk(x_sample, state_hgrn, state_conf, state_sconv, lbs, *weights)
    return (y_prompt, y_sample, h_p, c_p, s_p, h_s, c_s, s_s)
```

```python
import math
from contextlib import ExitStack
import numpy as np
import concourse.bass as bass
import concourse.mybir as mybir
from concourse.bass_utils import run_bass_kernel_spmd

F32 = mybir.dt.float32
BF16 = mybir.dt.bfloat16
AF = mybir.ActivationFunctionType
ALU = mybir.AluOpType
AX = mybir.AxisListType

NCORES = 8
D = 1024
KC = 8
DFF = 2816
NJ = 22
NJG = 11
EIN = 3072
DEPTH = 4
ALPHA = (2 * DEPTH) ** 0.25
LN_EPS = 1e-5
RMS_EPS = 1e-6
QSCALE = 128 ** -0.5
SW = 1040
NTOK = 2064
NB = 16

PV_LNMW, PV_LNMB, PV_LNFW, PV_LNFB = 0, 32, 64, 96
PV_DWW = 128
PV_DWB = 376
PV_CLW = 384
PV_CLB = 392
PV_SCW = 400
PV_LBL = 448
PV_GNW = 456
NPV = 460
C_ID = 0
C_TRI = 128
C_SM = 128
C_ONE = 640
NCST = 768


class Prog:
    ENG = ("pe", "act", "dve", "pool", "sp")

    def __init__(self, nc, es):
        self.nc = nc
        self.es = es
        self.q = {e: [] for e in self.ENG}
        self.semh = {e: es.enter_context(nc.semaphore("s_" + e)) for e in self.ENG}
        self.cnt = {e: 0 for e in self.ENG}
        self.seen = {e: {} for e in self.ENG}
        self.lastw = {}
        self.readers = {}
        self.dtot = {}
        self.dbar = {}

    def _deps(self, eng, reads, writes):
        deps = []
        for b in reads:
            if b in self.lastw:
                deps.append(self.lastw[b])
        for b in writes:
            if b in self.lastw:
                deps.append(self.lastw[b])
            deps.extend(self.readers.get(b, ()))
        out = []
        for (sk, val) in deps:
            if sk == "pe" and eng == "pe":
                continue
            if self.seen[eng].get(sk, 0) >= val:
                continue
            self.seen[eng][sk] = val
            out.append((sk, val))
        return out

    def _book(self, me, reads, writes):
        for b in writes:
            self.lastw[b] = me
            self.readers[b] = []
        for b in reads:
            self.readers.setdefault(b, []).append(me)

    def op(self, eng, fn, reads=(), writes=(), signal=True):
        waits = self._deps(eng, reads, writes)
        if signal:
            self.cnt[eng] += 1
            me = (eng, self.cnt[eng])
        else:
            me = (eng, self.cnt[eng] + 1)
        self.q[eng].append((waits, fn, "sig" if signal else None))
        self._book(me, reads, writes)

    def dma(self, eng, fn, sem, reads=(), writes=(), bar=False):
        waits = self._deps(eng, reads, writes)
        key = "d:" + sem
        if key not in self.semh:
            self.semh[key] = self.es.enter_context(self.nc.semaphore("d_" + sem))
            self.dtot[key] = 0
        self.dtot[key] += 16
        self.dbar[key] = bar
        me = (key, self.dtot[key])
        if eng == "sp":
            hist = self.__dict__.setdefault("sp_hist", [])
            if len(hist) >= 2:
                pk, pv_ = hist[-2]
                if pk != key and self.seen[eng].get(pk, 0) < pv_:
                    self.seen[eng][pk] = pv_
                    waits.append((pk, pv_))
            hist.append(me)
        self.q[eng].append((waits, fn, key))
        self._book(me, reads, writes)

    def barrier(self):
        snap = dict(self.cnt)
        dsn = {k: v for k, v in self.dtot.items() if self.dbar.get(k)}
        for e in self.ENG:
            waits = []
            for e2 in self.ENG:
                if e2 == e or snap[e2] == 0:
                    continue
                if self.seen[e].get(e2, 0) < snap[e2]:
                    self.seen[e][e2] = snap[e2]
                    waits.append((e2, snap[e2]))
            for k, v in dsn.items():
                if self.seen[e].get(k, 0) < v:
                    self.seen[e][k] = v
                    waits.append((k, v))
            if waits:
                self.q[e].append((waits, None, None))
        for b in list(self.lastw.keys()):
            if not self.lastw[b][0].startswith("d:"):
                del self.lastw[b]
        for b in list(self.readers.keys()):
            self.readers[b] = [r for r in self.readers[b] if r[0].startswith("d:") and not self.dbar.get(r[0])]

    def finish(self):
        waits = [(k, v) for k, v in self.dtot.items()]
        waits += [(e, self.cnt[e]) for e in self.ENG if e != "sp" and self.cnt[e] > 0]
        self.q["sp"].append((waits, None, None))

    def emit(self, block):
        for e, attr in (("pe", "tensor"), ("act", "scalar"), ("dve", "vector"), ("pool", "gpsimd"), ("sp", "sync")):
            def body(engobj, e=e):
                for waits, fn, sig in self.q[e]:
                    for (sk, val) in waits:
                        engobj.wait_ge(self.semh[sk], val)
                    if fn is None:
                        continue
                    ins = fn(engobj)
                    if sig == "sig":
                        ins.then_inc(self.semh[e], 1)
                    elif sig is not None:
                        ins.then_inc(self.semh[sig], 16)
            getattr(block, attr)(body)


def build(NL=DEPTH):
    nc = bass.Bass("TRN2", target_bir_lowering=False)

    def din(name, shape):
        return nc.dram_tensor(name, shape, F32, kind="ExternalInput").ap()

    def dout(name, shape):
        return nc.dram_tensor(name, shape, F32, kind="ExternalOutput").ap()

    xT = din("xT", [D, NTOK])
    s_hgrn = din("s_hgrn", [2, 128, NB, 4, 128])
    s_conf = din("s_conf", [2, 512, NB, 30])
    s_sconv = din("s_sconv", [2, D, NB, 2])
    w_in_even = din("w_in_even", [2, D, EIN])
    w_out_even = din("w_out_even", [2, D, D])
    sc_w_in = din("sc_w_in", [2, D, EIN])
    sc_w_out = din("sc_w_out", [2, D, D])
    ffn_w1 = din("ffn_w1", [DEPTH, D, DFF])
    ffn_w3 = din("ffn_w3", [DEPTH, D, DFF])
    ffn_w2 = din("ffn_w2", [DEPTH, DFF, D])
    pv_d = din("pv", [128, NPV])
    cstf_d = din("cstf", [128, NCST])
    cstb_d = din("cstb", [128, NCST])

    yT = dout("yT", [D, NTOK])
    o_hgrn_p = dout("o_hgrn_p", [2, 4, 128, 128])
    o_conf_p = dout("o_conf_p", [2, 512, 30])
    o_sconv_p = dout("o_sconv_p", [2, D, 2])
    o_hgrn_s = dout("o_hgrn_s", [2, 128, NB, 4, 128])
    o_conf_s = dout("o_conf_s", [2, 512, NB, 30])
    o_sconv_s = dout("o_sconv_s", [2, D, NB, 2])

    es = ExitStack()
    with es:
        def sb(name, shape, dt=F32):
            return es.enter_context(nc.sbuf_tensor(name, shape, dt))

        XF = sb("XF", [128, KC, SW])
        XB = sb("XB", [128, KC, SW], BF16)
        PV = sb("PV", [128, NPV])
        CF = sb("CF", [128, NCST])
        CB = sb("CB", [128, NCST], BF16)
        LBV = sb("LBV", [128, 3, 2, 4])
        ZST = sb("ZST", [128, 2, 4, 128])
        SBF = sb("SBF", [128, 4, 128], BF16)
        EBA = sb("EBA", [128, 2, 4, 9])
        UT = sb("UT", [128, 2, 4, 30])
        ZT = sb("ZT", [128, 2, 8, 2])
        SFIN = sb("SFIN", [128, 4, 128])
        WIN = sb("WIN", [128, KC, EIN], BF16)
        W2O = sb("W2O", [128, 8192], BF16)
        SCRB = 74240
        SCR = sb("SCR", [128, SCRB // 4])
        SCRH = SCR.bitcast(BF16)
        PS = es.enter_context(nc.psum_tensor("PS", [128, 4096], F32))
        PSH = PS.bitcast(BF16)

        pr = Prog(nc, es)

        IDB = CB[:, C_ID:C_ID + 128]
        ONEB = CB[:, C_ONE:C_ONE + 128]
        TRI = CB[:, C_TRI:C_TRI + 512]
        SMASK = CF[:, C_SM:C_SM + 512]
        IDF = CF[:, C_ID:C_ID + 128]
        ONEF = CF[:, C_ONE:C_ONE + 128]

        off = [0]

        def carve(nbytes):
            o = off[0]
            off[0] += (nbytes + 63) // 64 * 64
            return o

        def vf(o, n):
            return SCR[:, o // 4:o // 4 + n]

        def vh(o, n):
            return SCRH[:, o // 2:o // 2 + n]

        T = [vf(carve(2048), 512) for _ in range(6)]
        o_alias = off[0]
        ACC = vf(carve(8192), 2048).rearrange("p (a b) -> p a b", a=4)
        QT = vh(carve(4096), 2048).rearrange("p (a b) -> p a b", a=4)
        KT = vh(carve(4096), 2048).rearrange("p (a b) -> p a b", a=4)
        VT = vh(carve(4096), 2048).rearrange("p (a b) -> p a b", a=4)
        SG = vh(carve(4096), 2048).rearrange("p (a b) -> p a b", a=4)
        KTT = vh(carve(4096), 2048)
        SC = vh(carve(2048), 1024)
        o_alias_end = off[0]
        M = vh(carve(8192), 4096).rearrange("p (a b) -> p a b", a=8)
        o_u = carve(8704)
        UREG = vf(o_u, 4 * 544)
        U = vh(o_u, 4 * 542).rearrange("p (a b) -> p a b", a=4)
        DG = [vh(o_u + 4352 + i * 256, 128) for i in range(16)]
        o_ln = off[0]
        RBT = [vh(carve(1024), 512) for _ in range(2)]
        R2T = [vh(carve(1024), 512) for _ in range(2)]
        MU = vf(carve(2048), 512)
        RS = vf(carve(2048), 512)
        MSQ = vf(carve(2048), 512)
        VSB = vf(carve(2048), 512)
        VMB = vf(carve(2048), 512)
        assert off[0] <= SCRB, off[0]
        off[0] = o_alias
        SRF = [vf(carve(8192), 2048) for _ in range(2)]
        SR = [x.rearrange("p (q a b) -> p q a b", q=4, a=4) for x in SRF]
        UHF = vf(carve(7680), 4 * 16 * 30)
        UH = UHF.rearrange("p (a b c) -> p a b c", a=4, b=16)
        TMPC = vf(carve(1920), 16 * 30).rearrange("p (b c) -> p b c", b=16)
        KV = vf(carve(2048), 512).rearrange("p (a b) -> p a b", a=4)
        PZS = vf(carve(1536), 384).rearrange("p (a b) -> p a b", a=24)
        POS = vf(carve(256), 64).rearrange("p (a b) -> p a b", a=4)
        assert off[0] <= o_alias_end, (off[0], o_alias_end)
        off[0] = o_alias
        ZB = vf(carve(8 * 514 * 4), 8 * 514).rearrange("p (a b) -> p a b", a=8)
        assert off[0] <= o_alias_end
        off[0] = 0
        H = vh(carve(NJG * SW * 2), NJG * SW).rearrange("p (a b) -> p a b", a=NJG)
        W1S = [vh(carve(8192), 4096).rearrange("p (a b) -> p a b", a=8) for _ in range(2)]
        W3S = [vh(carve(8192), 4096).rearrange("p (a b) -> p a b", a=8) for _ in range(2)]
        FT = [vf(carve(2048), 512) for _ in range(2)]
        o_ffn_end = off[0]
        assert o_ffn_end <= o_ln
        WOUT = W2O[:, 0:8192].rearrange("p (a b) -> p a b", a=8)
        W2S = [W2O[:, i * 2816:(i + 1) * 2816].rearrange("p (a b) -> p a b", a=NJG) for i in range(2)]

        def bank(b, n=512, p0=0, p1=128):
            return PS[p0:p1, b * 512:b * 512 + n]

        def act(out, in_, func, reads, writes, scale=None, bias=None):
            kw = {}
            if scale is not None:
                kw["scale"] = scale
            if bias is not None:
                kw["bias"] = bias
            pr.op("act", lambda e: e.activation(out=out, in_=in_, func=func, **kw), reads, writes)

        def tt(out, a, b, op, reads, writes):
            pr.op("dve", lambda e: e.tensor_tensor(out=out, in0=a, in1=b, op=op), reads, writes)

        def ts(out, a, s1, s2, op0, op1, reads, writes):
            pr.op("dve", lambda e: e.tensor_scalar(out=out, in0=a, scalar1=s1, scalar2=s2, op0=op0, op1=op1),
                  reads, writes)

        def ts1(out, a, s1, op0, reads, writes):
            pr.op("dve", lambda e: e.tensor_scalar(out=out, in0=a, scalar1=s1, scalar2=None, op0=op0), reads, writes)

        def stt(out, a, s, b, op0, op1, reads, writes):
            pr.op("dve", lambda e: e.scalar_tensor_tensor(out=out, in0=a, scalar=s, in1=b, op0=op0, op1=op1),
                  reads, writes)

        def mm(out, lhsT, rhs, start, stop, reads, writes, signal):
            pr.op("pe", lambda e: e.matmul(out, lhsT=lhsT, rhs=rhs, start=start, stop=stop), reads, writes, signal)

        def mm_group(out, pairs, reads, wkey):
            n = len(pairs)
            for i, (l, r) in enumerate(pairs):
                mm(out, l, r, i == 0, i == n - 1, reads, [wkey], i == n - 1)

        cst_biases = {}

        def fbias(val):
            if val not in cst_biases:
                cst_biases[val] = len(cst_biases)
            return BIASC[:, cst_biases[val]:cst_biases[val] + 1]

        BIASC = sb("BIASC", [128, 4])

        pr.dma("sp", lambda e: e.dma_start(out=PV[:, :], in_=pv_d[:, :]), "pv", writes=["PV"])
        pr.dma("sp", lambda e: e.dma_start(out=CF[:, :], in_=cstf_d[:, :]), "cf", writes=["CF"])
        pr.dma("pool", lambda e: e.dma_start(out=CB[:, :], in_=cstb_d[:, :]), "cb", writes=["CB"])
        pr.op("dve", lambda e: e.memset(ZST[:, :, :, :], 0.0), writes=[("Z%d" % e_, h_) for e_ in range(2) for h_ in range(4)])
        pr.op("dve", lambda e: e.memset(EBA[:, :, :, :], 1.0), writes=["EBA0", "EBA1"])
        pr.op("dve", lambda e: e.memset(UT[:, :, :, :], 0.0), writes=["UT0", "UT1"])
        pr.op("dve", lambda e: e.memset(ZT[:, :, :, :], 0.0), writes=["ZT0", "ZT1"])
        pr.op("dve", lambda e: e.memset(SBF[:, :, :], 0.0), writes=[("SBF", h_) for h_ in range(4)])
        pr.op("dve", lambda e: e.memset(BIASC[:, 0:1], LN_EPS), writes=["BIASC"])
        pr.op("dve", lambda e: e.memset(BIASC[:, 1:2], RMS_EPS), writes=["BIASC"])
        cst_biases[LN_EPS] = 0
        cst_biases[RMS_EPS] = 1
        pr.op("dve", lambda e: e.memset(LBV[:, 0, 0, :], 0.0), writes=["LBV"])
        tt(LBV[:, 0, 1, :], PV[:, PV_LBL + 4:PV_LBL + 8], PV[:, PV_LBL:PV_LBL + 4], ALU.subtract, ["PV"], ["LBV"])
        act(LBV[:, 0, 1, :], LBV[:, 0, 1, :], AF.Sigmoid, ["LBV"], ["LBV"])
        ts(LBV[:, 1, :, :], LBV[:, 0, :, :], -1.0, 1.0, ALU.mult, ALU.add, ["LBV"], ["LBV"])
        ts1(LBV[:, 2, :, :], LBV[:, 1, :, :], -1.0, ALU.mult, ["LBV"], ["LBV"])

        win_src = []
        wout_src = []
        for l in range(DEPTH):
            if l % 2 == 0:
                win_src.append(w_in_even[l // 2])
                wout_src.append(w_out_even[l // 2])
            else:
                win_src.append(sc_w_in[l // 2])
                wout_src.append(sc_w_out[l // 2])

        def issue_win_piece(l, i):
            src = win_src[l].rearrange("(kc p) n -> p kc n", p=128)[:, :, i * 1024:(i + 1) * 1024]
            dst = WIN[:, :, i * 1024:(i + 1) * 1024]
            pr.dma("pool", lambda e: e.dma_start(out=dst, in_=src), "win%d" % i, writes=[("WIN", i)])

        def issue_wout(l):
            src = wout_src[l].rearrange("(kc p) n -> p kc n", p=128)
            pr.dma("pool", lambda e, s=src: e.dma_start(out=WOUT[:, :, :], in_=s), "wout",
                   writes=[("W2O", 0), ("W2O", 1)])

        def ln_stats(srcs, N, inv_n, eps, bm, bv):
            n = len(srcs)
            for i, (ap, key) in enumerate(srcs):
                rb = RBT[i % 2][:, 0:N]
                r2 = R2T[i % 2][:, 0:N]
                act(rb, ap, AF.Copy, [key], [("RBT", i % 2)])
                act(r2, ap, AF.Square, [key], [("R2T", i % 2)])
                mm(bank(bm, N), ONEB, rb, i == 0, i == n - 1, [("RBT", i % 2), "CB"], [("ps", bm)], True)
                mm(bank(bv, N), ONEB, r2, i == 0, i == n - 1, [("R2T", i % 2), "CB"], [("ps", bv)], True)
            ts1(MU[:, 0:N], bank(bm, N), inv_n, ALU.mult, [("ps", bm)], ["MU"])
            tt(MSQ[:, 0:N], MU[:, 0:N], MU[:, 0:N], ALU.mult, ["MU"], ["MSQ"])
            stt(MSQ[:, 0:N], bank(bv, N), inv_n, MSQ[:, 0:N], ALU.mult, ALU.subtract, [("ps", bv), "MSQ"], ["MSQ"])
            act(MSQ[:, 0:N], MSQ[:, 0:N], AF.Ln, ["MSQ", "BIASC"], ["MSQ"], bias=fbias(eps))
            act(RS[:, 0:N], MSQ[:, 0:N], AF.Exp, ["MSQ"], ["RS"], scale=-0.5)

        def deepnorm_ln(ti, c0, N, wcol, bcol, out_cols):
            srcs = [(XF[:, oc, c0:c0 + N], ("XF", ti, oc)) for oc in range(8)]
            ln_stats(srcs, N, 1.0 / D, LN_EPS, 6, 7)
            for oc in range(8):
                x = XF[:, oc, c0:c0 + N]
                k = ("XF", ti, oc)
                tt(x, x, MU[:, 0:N], ALU.subtract, [k, "MU"], [k])
                tt(x, x, RS[:, 0:N], ALU.mult, [k, "RS"], [k])
                act(x, x, AF.Identity, [k, "PV"], [k], scale=PV[:, wcol + oc:wcol + oc + 1],
                    bias=PV[:, bcol + oc:bcol + oc + 1])
                act(XB[:, oc, c0:c0 + N], x, AF.Copy, [k], [("XB", ti)])
            if out_cols is not None:
                dst = yT.rearrange("(kc p) t -> p kc t", p=128)[:, :, out_cols:out_cols + N]
                pr.dma("sp", lambda e: e.dma_start(out=dst, in_=XF[:, :, c0:c0 + N]), "y%d" % ti,
                       reads=[("XF", ti, oc) for oc in range(8)])

        pending_ln = []

        def out_proj_ln(l, ti, c0, N):
            for oc in range(8):
                b = oc % 4
                mm_group(bank(b, N), [(WOUT[:, kc, oc * 128:(oc + 1) * 128], M[:, kc, 0:N]) for kc in range(8)],
                         [("W2O", 0), "M"], ("ps", b))
                x = XF[:, oc, c0:c0 + N]
                stt(x, x, ALPHA, bank(b, N), ALU.mult, ALU.add, [("XF", ti, oc), ("ps", b)], [("XF", ti, oc)])
            if defer_flag[0]:
                pending_ln.append(lambda: deepnorm_ln(ti, c0, N, PV_LNMW + l * 8, PV_LNMB + l * 8, None))
            else:
                deepnorm_ln(ti, c0, N, PV_LNMW + l * 8, PV_LNMB + l * 8, None)

        defer_flag = [False]
        rot = [0]

        def nb(nbanks=4, base=0):
            rot[0] = (rot[0] + 1) % nbanks
            return base + rot[0]

        def inproj_fm(oc, c0, N, ti, b, col=0):
            out = PS[:, b * 512 + col:b * 512 + col + N]
            mm_group(out, [(WIN[:, kc, oc * 128:(oc + 1) * 128], XB[:, kc, c0:c0 + N]) for kc in range(8)],
                     [("WIN", oc // 8), ("XB", ti)], ("ps", b))

        def conf_ln_silu(e, N, accv):
            srcs = [(accv[:, cc, 0:N], "ACC") for cc in range(4)]
            ln_stats(srcs, N, 1.0 / 512, LN_EPS, 6, 7)
            for cc in range(4):
                a = accv[:, cc, 0:N]
                tt(a, a, MU[:, 0:N], ALU.subtract, ["ACC", "MU"], ["ACC"])
                tt(a, a, RS[:, 0:N], ALU.mult, ["ACC", "RS"], ["ACC"])
                act(M[:, 4 + cc, 0:N], a, AF.Silu, ["ACC", "PV"], ["M"],
                    scale=PV[:, PV_CLW + e * 4 + cc:PV_CLW + e * 4 + cc + 1],
                    bias=PV[:, PV_CLB + e * 4 + cc:PV_CLB + e * 4 + cc + 1])

        def rms_gate(e, h, N, o_ps, o_key, sg_ap, sg_key, sbank):
            r2 = R2T[h % 2][:, 0:N]
            act(r2, o_ps, AF.Square, [o_key], [("R2T", h % 2)])
            mm(bank(sbank, N), ONEB, r2, True, True, [("R2T", h % 2), "CB"], [("ps", sbank)], True)
            act(RS[:, 0:N], bank(sbank, N), AF.Ln, [("ps", sbank), "BIASC"], ["RS"], scale=1.0 / 128, bias=fbias(RMS_EPS))
            act(RS[:, 0:N], RS[:, 0:N], AF.Exp, ["RS"], ["RS"], scale=-0.5)
            tt(MU[:, 0:N], o_ps, RS[:, 0:N], ALU.mult, [o_key, "RS"], ["MU"])
            stt(M[:, h, 0:N], MU[:, 0:N], PV[:, PV_GNW + e:PV_GNW + e + 1], sg_ap, ALU.mult, ALU.mult,
                ["MU", "PV", sg_key], ["M"])

        def even_prompt_tile(l, ti, c0, last_prompt):
            e = l // 2
            N = 512
            lb = lambda h: LBV[:, 0, e, h:h + 1]
            omlb = lambda h: LBV[:, 1, e, h:h + 1]
            nomlb = lambda h: LBV[:, 2, e, h:h + 1]
            act(U[:, :, 0:30], UT[:, e, :, :], AF.Copy, ["UT%d" % e], ["U"])
            for cc in range(4):
                ba, bg = 2 * (cc % 2), 2 * (cc % 2) + 1
                inproj_fm(16 + cc, c0, N, ti, ba)
                inproj_fm(20 + cc, c0, N, ti, bg)
                tmp = T[cc % 2][:, 0:N]
                act(tmp, bank(bg, N), AF.Sigmoid, [("ps", bg)], [("T", cc % 2)])
                tt(U[:, cc, 30:30 + N], bank(ba, N), tmp, ALU.mult, [("ps", ba), ("T", cc % 2)], ["U"])
                tt(UT[:, e, cc, :], bank(ba, N)[:, N - 30:N], tmp[:, N - 30:N], ALU.mult,
                   [("ps", ba), ("T", cc % 2)], ["UT%d" % e])
            wcol = lambda cc, k: PV[:, PV_DWW + (e * 4 + cc) * 31 + k:PV_DWW + (e * 4 + cc) * 31 + k + 1]

            dgi = [0]

            def conv_taps(k0, k1):
                for k in range(k0, k1):
                    for cc in range(4):
                        i = dgi[0] % 16
                        dgi[0] += 1
                        ts1(DG[i], IDB, wcol(cc, k), ALU.mult, ["CB", "PV"], [("DG", i)])
                        mm(bank(4 + cc, N), DG[i], U[:, cc, k:k + N], k == 0, k == 30, [("DG", i), "U"],
                           [("ps", 4 + cc)], True)
            for h in range(4):
                bq, bf_, bg = nb(), nb(), nb()
                inproj_fm(0 + h, c0, N, ti, bq)
                inproj_fm(4 + h, c0, N, ti, bf_)
                inproj_fm(12 + h, c0, N, ti, bg)
                conv_taps(8 * h, min(8 * h + 8, 31) if h < 3 else 31)
                t1, t2, t3, t4, t5, t6 = [T[i][:, 0:N] for i in range(6)]
                act(t1, bank(bq, N), AF.Sigmoid, [("ps", bq)], [("T", 0)])
                act(t4, bank(bg, N), AF.Sigmoid, [("ps", bg)], [("T", 3)])
                act(t2, bank(bf_, N), AF.Sigmoid, [("ps", bf_)], [("T", 1)])
                tt(t1, bank(bq, N), t1, ALU.mult, [("ps", bq), ("T", 0)], [("T", 0)])
                tt(SG[:, h, :], bank(bg, N), t4, ALU.mult, [("ps", bg), ("T", 3)], ["SG"])
                ts(t3, t2, nomlb(h), omlb(h), ALU.mult, ALU.add, [("T", 1), "LBV"], [("T", 2)])
                act(t2, t2, AF.Ln, [("T", 1), "LBV"], [("T", 1)], scale=omlb(h), bias=lb(h))
                pr.op("dve", lambda en, t4=t4, t2=t2: en.tensor_tensor_scan(
                    out=t4, data0=SMASK[:, 0:N], data1=t2, initial=0.0, op0=ALU.mult, op1=ALU.add),
                    [("T", 1), "CF"], [("T", 3)])
                act(t5, t4, AF.Exp, [("T", 3)], [("T", 4)])
                act(t6, t4, AF.Exp, [("T", 3)], [("T", 5)], scale=-1.0)
                stt(QT[:, h, :], t1, QSCALE, t5, ALU.mult, ALU.mult, [("T", 0), ("T", 4)], ["QT"])
                tt(KT[:, h, :], t3, t6, ALU.mult, [("T", 2), ("T", 5)], ["KT"])
                act(EBA[:, e, h, 1:9], T[4][:, 63:512:64], AF.Copy, [("T", 4)], ["EBA%d" % e])
            for cc in range(4):
                act(ACC[:, cc, 0:N], bank(4 + cc, N), AF.Identity, [("ps", 4 + cc), "PV"], ["ACC"],
                    bias=PV[:, PV_DWB + e * 4 + cc:PV_DWB + e * 4 + cc + 1])
            for sub in range(4):
                b = nb()
                mm_group(bank(b), [(XB[:, kc, c0 + sub * 128:c0 + (sub + 1) * 128], WIN[:, kc, 1024:1536])
                                   for kc in range(8)], [("WIN", 1), ("XB", ti)], ("ps", b))
                act(VT[:, sub, :], bank(b), AF.Copy, [("ps", b)], ["VT"])
            for h in range(4):
                bs = 4 + h // 2
                bt = 6 + h // 2
                for c in range(8):
                    hp = (c % 2) * 64
                    col = bs * 512 + (h % 2) * 256 + (c // 2) * 64
                    mm(PS[hp:hp + 64, col:col + 64], KT[:, h, c * 64:(c + 1) * 64], QT[:, h, c * 64:(c + 1) * 64],
                       True, True, ["KT", "QT"], [("ps", bs)], c == 7)
                for c in range(8):
                    hp = (c % 2) * 64
                    col = bt * 1024 + (h % 2) * 512 + (c // 2) * 128
                    pr.op("pe", lambda en, hp=hp, col=col, h=h, c=c: en.transpose(
                        out=PSH[hp:hp + 64, col:col + 128], in_=KT[:, h, c * 64:(c + 1) * 64], identity=IDB),
                        ["KT", "CB"], [("ps", bt)], c == 7)
            for i in range(2):
                tt(SC[:, i * 512:(i + 1) * 512], bank(4 + i), TRI, ALU.mult, [("ps", 4 + i), "CB"], ["SC"])
                act(KTT[:, i * 1024:(i + 1) * 1024], PSH[:, (6 + i) * 1024:(7 + i) * 1024], AF.Copy,
                    [("ps", 6 + i)], ["KTT"])
            conf_ln_silu(e, N, ACC)
            if last_prompt:
                dstc = o_conf_p[e].rearrange("(cc p) k -> p cc k", p=128)
                pr.dma("sp", lambda en: en.dma_start(out=dstc, in_=UT[:, e, :, :]), "ocp", reads=["UT%d" % e])
            zk = "Z%d" % e
            for h in range(4):
                act(SBF[:, h, :], ZST[:, e, h, :], AF.Identity, [(zk, h), "EBA%d" % e], [("SBF", h)],
                    scale=EBA[:, e, h, 0:1])
            for c in range(8):
                hp = (c % 2) * 64
                sub = c // 2
                for h in range(4):
                    o_out = PS[:, h * 512 + c * 64:h * 512 + (c + 1) * 64]
                    mm(o_out, VT[hp:hp + 64, sub, h * 128:(h + 1) * 128],
                       SC[hp:hp + 64, h * 256 + sub * 64:h * 256 + (sub + 1) * 64], True, False,
                       ["VT", "SC"], [("ps", h)], False)
                    mm(o_out, SBF[:, h, :], QT[:, h, c * 64:(c + 1) * 64], False, True,
                       [("SBF", h), "QT"], [("ps", h)], True)
                    bu = 4 + h
                    mm(PS[:, bu * 512:bu * 512 + 128],
                       KTT[hp:hp + 64, (h * 4 + sub) * 128:(h * 4 + sub + 1) * 128],
                       VT[hp:hp + 64, sub, h * 128:(h + 1) * 128], True, True, ["KTT", "VT"], [("ps", bu)], True)
                    z = ZST[:, e, h, :]
                    stt(z, z, EBA[:, e, h, c:c + 1], PS[:, bu * 512:bu * 512 + 128],
                        ALU.mult, ALU.add, [(zk, h), "EBA%d" % e, ("ps", bu)], [(zk, h)])
                    act(SBF[:, h, :], z, AF.Identity, [(zk, h), "EBA%d" % e], [("SBF", h)], scale=EBA[:, e, h, c + 1:c + 2])
            if last_prompt:
                for h in range(4):
                    ts1(SFIN[:, h, :], ZST[:, e, h, :], EBA[:, e, h, 8:9], ALU.mult, [(zk, h), "EBA%d" % e], ["SFIN"])
                dst = o_hgrn_p[e].rearrange("h k v -> k h v")
                pr.dma("sp", lambda en: en.dma_start(out=dst, in_=SFIN[:, :, :]), "ohp", reads=["SFIN"], writes=[])
            act(EBA[:, e, :, 0], EBA[:, e, :, 8], AF.Copy, ["EBA%d" % e], ["EBA%d" % e])
            for h in range(4):
                rms_gate(e, h, N, bank(h), ("ps", h), SG[:, h, :], "SG", 6 + h % 2)
            out_proj_ln(l, ti, c0, N)

        def even_sample_tile(l, ti, c0):
            e = l // 2
            N = NB
            PZP = PS[:, 0:24 * N].rearrange("p (a b) -> p a b", a=24)
            for oc in list(range(0, 8)) + list(range(12, 24)):
                inproj_fm(oc, c0, N, ti, 0, col=oc * N)
            act(PZS[:, 0:8, :], PZP[:, 0:8, :], AF.Copy, [("ps", 0)], ["PZS"])
            act(PZS[:, 12:24, :], PZP[:, 12:24, :], AF.Copy, [("ps", 0)], ["PZS"])
            PZ = PZS
            mm_group(PS[0:N, 512:1024], [(XB[:, kc, c0:c0 + N], WIN[:, kc, 1024:1536]) for kc in range(8)],
                     [("WIN", 1), ("XB", ti)], ("ps", 1))
            small = lambda i: T[i][:, 0:4 * N].rearrange("p (a b) -> p a b", a=4)
            QS, FS, KK, SGS, TMPS, ACS = [small(i) for i in range(6)]
            VS = VSB[0:N, 0:512]
            VM = VMB[0:N, 0:512]
            act(QS, PZ[:, 0:4, :], AF.Sigmoid, ["PZS"], [("T", 0)])
            stt(QS, PZ[:, 0:4, :], QSCALE, QS, ALU.mult, ALU.mult, ["PZS", ("T", 0)], [("T", 0)])
            act(TMPS, PZ[:, 4:8, :], AF.Sigmoid, ["PZS"], [("T", 4)])
            for h in range(4):
                ts(FS[:, h, :], TMPS[:, h, :], LBV[:, 1, e, h:h + 1], LBV[:, 0, e, h:h + 1], ALU.mult, ALU.add,
                   [("T", 4), "LBV"], [("T", 1)])
            ts(KK, FS, -1.0, 1.0, ALU.mult, ALU.add, [("T", 1)], [("T", 2)])
            act(SGS, PZ[:, 12:16, :], AF.Sigmoid, ["PZS"], [("T", 3)])
            tt(SGS, PZ[:, 12:16, :], SGS, ALU.mult, ["PZS", ("T", 3)], [("T", 3)])
            act(VS, PS[0:N, 512:1024], AF.Copy, [("ps", 1)], ["VSB"])
            OUTUF = UREG[:, 0:4 * 16 * 30]
            OUTU = OUTUF.rearrange("p (a b c) -> p a b c", a=4, b=16)
            UN = T[4][:, 64:64 + 4 * N].rearrange("p (a b) -> p a b", a=4)
            srcu = s_conf[e].rearrange("(cc p) b k -> p cc (b k)", p=128)
            pr.dma("sp", lambda en: en.dma_start(out=UHF.rearrange("p (a b) -> p a b", a=4), in_=srcu), "ust",
                   writes=["UH"], bar=True)
            act(TMPS, PZ[:, 20:24, :], AF.Sigmoid, ["PZS"], [("T", 4)])
            tt(UN, PZ[:, 16:20, :], TMPS, ALU.mult, ["PZS", ("T", 4)], ["UN"])
            for cc in range(4):
                wap = PV[:, PV_DWW + (e * 4 + cc) * 31:PV_DWW + (e * 4 + cc) * 31 + 30]
                wbc = bass.AP(wap.tensor, wap.offset, [list(wap.ap[0]), [0, N], [1, 30]])
                w30 = PV[:, PV_DWW + (e * 4 + cc) * 31 + 30:PV_DWW + (e * 4 + cc) * 31 + 31]
                tt(TMPC, UH[:, cc, :, :], wbc, ALU.mult, ["UH", "PV"], ["TMPC"])
                pr.op("dve", lambda en, cc=cc: en.tensor_reduce(out=ACS[:, cc, :], in_=TMPC, axis=AX.X, op=ALU.add),
                      ["TMPC"], ["ACC"])
                stt(ACS[:, cc, :], UN[:, cc, :], w30, ACS[:, cc, :], ALU.mult, ALU.add, ["UN", "PV", "ACC"], ["ACC"])
                ts1(ACS[:, cc, :], ACS[:, cc, :], PV[:, PV_DWB + e * 4 + cc:PV_DWB + e * 4 + cc + 1], ALU.add,
                    ["ACC", "PV"], ["ACC"])
            act(OUTU[:, :, :, 0:29], UH[:, :, :, 1:30], AF.Copy, ["UH"], ["U"])
            act(OUTU[:, :, :, 29], UN, AF.Copy, ["UN"], ["U"])
            dstu = o_conf_s[e].rearrange("(cc p) b k -> p cc (b k)", p=128)
            pr.dma("sp", lambda en: en.dma_start(out=dstu, in_=OUTUF.rearrange("p (a b) -> p a b", a=4)), "ocs",
                   reads=["U"], bar=True)
            conf_ln_silu(e, N, ACS)
            PO = PS[:, 1024:1024 + 4 * N].rearrange("p (a b) -> p a b", a=4)
            VMS = [VMB[0:N, 0:512], MSQ[0:N, 0:512]]
            VMK = ["VMB", "MSQ"]

            def mk_vm(b):
                ts1(VMS[b % 2], VS, IDF[0:N, b:b + 1], ALU.mult, ["VSB", "CF"], [VMK[b % 2]])

            def mk_vbc(b):
                bx = 3 + (b % 2)
                mm(bank(bx), ONEF[0:N, :], VMS[b % 2], True, True, [VMK[b % 2], "CF"], [("ps", bx)], True)

            mk_vm(0)
            mk_vm(1)
            mk_vbc(0)
            for b in range(NB):
                q, bb = b // 4, b % 4
                sl = q % 2
                if bb == 0:
                    src = s_hgrn[e][:, 4 * q:4 * q + 4, :, :].rearrange("p q a b -> p (q a b)")
                    pr.dma("sp", lambda en, s=src, sl=sl: en.dma_start(out=SRF[sl], in_=s), "sr%d" % sl,
                           writes=[("SR", sl)], bar=True)
                if b + 1 < NB:
                    mk_vbc(b + 1)
                if b + 2 < NB:
                    mk_vm(b + 2)
                bx = 3 + (b % 2)
                for h in range(4):
                    ts1(KV[:, h, :], bank(bx)[:, h * 128:(h + 1) * 128], KK[:, h, b:b + 1], ALU.mult,
                        [("ps", bx), ("T", 2)], [("KV", h)])
                    stt(SR[sl][:, bb, h, :], SR[sl][:, bb, h, :], FS[:, h, b:b + 1], KV[:, h, :], ALU.mult, ALU.add,
                        [("SR", sl), ("T", 1), ("KV", h)], [("SR", sl)])
                for h in range(4):
                    mm(PO[:, h, b:b + 1], SR[sl][:, bb, h, :], QS[:, h, b:b + 1], True, True,
                       [("SR", sl), ("T", 0)], [("ps", 2)], True)
                if bb == 3:
                    dst = o_hgrn_s[e][:, 4 * q:4 * q + 4, :, :].rearrange("p q a b -> p (q a b)")
                    pr.dma("sp", lambda en, d=dst, sl=sl: en.dma_start(out=d, in_=SRF[sl]), "so%d" % sl,
                           reads=[("SR", sl)], bar=True)
            act(POS, PO, AF.Copy, [("ps", 2)], ["POS"])
            for h in range(4):
                rms_gate(e, h, N, POS[:, h, :], "POS", SGS[:, h, :], ("T", 3), 6 + h % 2)
            out_proj_ln(l, ti, c0, N)

        def odd_prompt_tile(l, ti, c0, last_prompt):
            o = l // 2
            N = 512
            act(ZB[:, :, 0:2], ZT[:, o, :, :], AF.Copy, ["ZT%d" % o], ["ZB"])
            for kc in range(8):
                ba, bc, bx = nb(6), nb(6), nb(6)
                inproj_fm(kc, c0, N, ti, ba)
                inproj_fm(8 + kc, c0, N, ti, bc)
                inproj_fm(16 + kc, c0, N, ti, bx)
                A = T[kc % 2][:, 0:N]
                Y = T[2 + kc % 2][:, 0:N]
                act(A, bank(bc, N), AF.Copy, [("ps", bc)], [("T", kc % 2)])
                z = ZB[:, kc, 2:2 + N]
                tt(z, bank(bx, N), A, ALU.mult, [("ps", bx), ("T", kc % 2)], [("ZB", kc)])
                w = lambda k: PV[:, PV_SCW + (o * 8 + kc) * 3 + k:PV_SCW + (o * 8 + kc) * 3 + k + 1]
                yk = ("T", 2 + kc % 2)
                ts1(Y, z, w(2), ALU.mult, [("ZB", kc), "PV"], [yk])
                stt(Y, ZB[:, kc, 1:1 + N], w(1), Y, ALU.mult, ALU.add, [("ZB", kc), "ZB", "PV", yk], [yk])
                stt(Y, ZB[:, kc, 0:N], w(0), Y, ALU.mult, ALU.add, [("ZB", kc), "ZB", "PV", yk], [yk])
                tt(M[:, kc, 0:N], bank(ba, N), Y, ALU.mult, [("ps", ba), yk], ["M"])
            act(ZT[:, o, :, :], ZB[:, :, N:N + 2], AF.Copy, ["ZB"] + [("ZB", kc) for kc in range(8)], ["ZT%d" % o])
            if last_prompt:
                dst = o_sconv_p[o].rearrange("(kc p) k -> p kc k", p=128)
                pr.dma("sp", lambda en: en.dma_start(out=dst, in_=ZT[:, o, :, :]), "osp", reads=["ZT%d" % o])
            out_proj_ln(l, ti, c0, N)

        def odd_sample_tile(l, ti, c0):
            o = l // 2
            N = NB
            PZP = PS[:, 0:24 * N].rearrange("p (a b) -> p a b", a=24)
            for oc in range(24):
                inproj_fm(oc, c0, N, ti, 0, col=oc * N)
            act(PZS, PZP, AF.Copy, [("ps", 0)], ["PZS"])
            PZ = PZS
            SS = T[0][:, 0:8 * N * 2].rearrange("p (a b c) -> p a b c", a=8, b=N)
            OS = T[1][:, 0:8 * N * 2].rearrange("p (a b c) -> p a b c", a=8, b=N)
            A = T[2][:, 0:8 * N].rearrange("p (a b) -> p a b", a=8)
            Zs = T[3][:, 0:8 * N].rearrange("p (a b) -> p a b", a=8)
            Y = T[4][:, 0:8 * N].rearrange("p (a b) -> p a b", a=8)
            src = s_sconv[o].rearrange("(kc p) b k -> p kc (b k)", p=128)
            pr.dma("sp", lambda en: en.dma_start(out=T[0][:, 0:8 * N * 2].rearrange("p (a b) -> p a b", a=8), in_=src),
                   "ssin", writes=[("T", 0)], bar=True)
            act(A, PZ[:, 8:16, :], AF.Copy, ["PZS"], [("T", 2)])
            tt(Zs, PZ[:, 16:24, :], A, ALU.mult, ["PZS", ("T", 2)], [("T", 3)])
            for kc in range(8):
                w = lambda k: PV[:, PV_SCW + (o * 8 + kc) * 3 + k:PV_SCW + (o * 8 + kc) * 3 + k + 1]
                ts1(Y[:, kc, :], Zs[:, kc, :], w(2), ALU.mult, [("T", 3), "PV"], [("T", 4)])
                stt(Y[:, kc, :], SS[:, kc, :, 1], w(1), Y[:, kc, :], ALU.mult, ALU.add, [("T", 0), "PV", ("T", 4)],
                    [("T", 4)])
                stt(Y[:, kc, :], SS[:, kc, :, 0], w(0), Y[:, kc, :], ALU.mult, ALU.add, [("T", 0), "PV", ("T", 4)],
                    [("T", 4)])
            tt(M[:, :, 0:N], PZ[:, 0:8, :], Y, ALU.mult, ["PZS", ("T", 4)], ["M"])
            act(OS[:, :, :, 0], SS[:, :, :, 1], AF.Copy, [("T", 0)], [("T", 1)])
            act(OS[:, :, :, 1], Zs, AF.Copy, [("T", 3)], [("T", 1)])
            dst = o_sconv_s[o].rearrange("(kc p) b k -> p kc (b k)", p=128)
            pr.dma("sp", lambda en: en.dma_start(out=dst, in_=T[1][:, 0:8 * N * 2].rearrange("p (a b) -> p a b", a=8)),
                   "ssout", reads=[("T", 1)], bar=True)
            out_proj_ln(l, ti, c0, N)

        def ffn_phase(l, tiles, next_win, is_last):
            pr.barrier()
            loads = []
            for g in range(2):
                for bi in range(3):
                    loads.append(("13", g, bi))
                for ob in range(4):
                    loads.append(("2", g, ob))
            issued = {"13": 0, "2": 0}
            seq = {"13": [x for x in loads if x[0] == "13"], "2": [x for x in loads if x[0] == "2"]}
            win_left = list(range(3)) if next_win is not None else []

            def issue(kind, idx):
                _, g, bi = seq[kind][idx]
                sl = idx % 2
                if kind == "13":
                    j0 = g * NJG + bi * 4
                    nj = min(4, g * NJG + NJG - j0)
                    for nm, wsrc, dstt in (("w1", ffn_w1, W1S), ("w3", ffn_w3, W3S)):
                        src = wsrc[l].rearrange("(kc p) n -> p kc n", p=128)[:, :, j0 * 128:(j0 + nj) * 128]
                        dst = dstt[sl][:, :, 0:nj * 128]
                        pr.dma("pool", lambda en, s=src, d=dst: en.dma_start(out=d, in_=s), "%s_%d" % (nm, sl),
                               writes=[(nm, sl)])
                else:
                    src = ffn_w2[l].rearrange("(j p) n -> p j n", p=128)[:, g * NJG:(g + 1) * NJG, bi * 256:(bi + 1) * 256]
                    dst = W2S[sl][:, :, :]
                    pr.dma("pool", lambda en, s=src, d=dst: en.dma_start(out=d, in_=s), "w2_%d" % sl,
                           writes=[("W2O", sl)])
                if win_left:
                    issue_win_piece(next_win, win_left.pop(0))

            def ensure(kind, idx):
                while issued[kind] <= min(idx + 1, len(seq[kind]) - 1):
                    issue(kind, issued[kind])
                    issued[kind] += 1

            ensure("13", 0)
            while pending_ln:
                pending_ln.pop(0)()
            i13 = 0
            i2 = 0
            pairs = [(0, 1), (2, 3), (4, 5), (6, 7)]
            pi = 0
            for g in range(2):
                for bi in range(3):
                    ensure("13", i13)
                    if bi == 1:
                        ensure("2", i2 - 1 if i2 > 0 else 0)
                    sl = i13 % 2
                    j0 = g * NJG + bi * 4
                    nj = min(4, g * NJG + NJG - j0)
                    for jj in range(nj):
                        jl = j0 + jj - g * NJG
                        for (ti, c0, N, dc, kind, gid) in tiles:
                            ba, bb = pairs[pi % 4]
                            pi += 1
                            mm_group(bank(ba, N), [(W1S[sl][:, kc, jj * 128:(jj + 1) * 128], XB[:, kc, c0:c0 + N])
                                                  for kc in range(8)], [("w1", sl), ("XB", ti)], ("ps", ba))
                            mm_group(bank(bb, N), [(W3S[sl][:, kc, jj * 128:(jj + 1) * 128], XB[:, kc, c0:c0 + N])
                                                  for kc in range(8)], [("w3", sl), ("XB", ti)], ("ps", bb))
                            ft = FT[pi % 2][:, 0:N]
                            act(ft, bank(ba, N), AF.Silu, [("ps", ba)], [("FT", pi % 2)])
                            tt(H[:, jl, c0:c0 + N], ft, bank(bb, N), ALU.mult, [("FT", pi % 2), ("ps", bb)],
                               [("H", ti)])
                    i13 += 1
                for ob in range(4):
                    ensure("2", i2)
                    sl = i2 % 2
                    for ol in range(2):
                        oc = ob * 2 + ol
                        for (ti, c0, N, dc, kind, gid) in tiles:
                            b = nb(6)
                            mm_group(bank(b, N), [(W2S[sl][:, jl, ol * 128:(ol + 1) * 128], H[:, jl, c0:c0 + N])
                                                 for jl in range(NJG)], [("W2O", sl), ("H", ti)], ("ps", b))
                            x = XF[:, oc, c0:c0 + N]
                            if g == 0:
                                stt(x, x, ALPHA, bank(b, N), ALU.mult, ALU.add, [("XF", ti, oc), ("ps", b)],
                                    [("XF", ti, oc)])
                            else:
                                tt(x, x, bank(b, N), ALU.add, [("XF", ti, oc), ("ps", b)], [("XF", ti, oc)])
                    i2 += 1
            while win_left:
                issue_win_piece(next_win, win_left.pop(0))
            pr.barrier()
            for (ti, c0, N, dc, kind, gid) in tiles:
                deepnorm_ln(ti, c0, N, PV_LNFW + l * 8, PV_LNFB + l * 8, dc if is_last else None)

        segs = [
            [(0, 0, 512, 0, "p", 0), (1, 512, 512, 512, "p", 1), (2, 1024, NB, 2048, "s", 4)],
            [(0, 0, 512, 1024, "p", 2), (1, 512, 512, 1536, "p", 3)],
        ]
        LAYERS = list(range(NL)) if isinstance(NL, int) else list(NL)
        import os as _os
        for _i in range(int(_os.environ.get("DUMMY_DVE", "0"))):
            pr.op("dve", lambda e: e.memset(BIASC[:, 3:4], 0.0), writes=["dummy"])
        for _i in range(int(_os.environ.get("DUMMY_ACT", "0"))):
            pr.op("act", lambda e: e.activation(out=BIASC[:, 2:3], in_=BIASC[:, 0:1], func=AF.Copy), writes=["dummy2"])
        for _i in range(int(_os.environ.get("DUMMY_SP", "0"))):
            pr.dma("sp", lambda e: e.dma_start(out=PV[:, 458:460], in_=pv_d[:, 458:460]), "dummysp", writes=["dummy3"])
        for _i in range(int(_os.environ.get("DUMMY_POOL", "0"))):
            pr.dma("pool", lambda e: e.dma_start(out=PV[:, 456:458], in_=pv_d[:, 456:458]), "dummypool", writes=["dummy4"])
        if _os.environ.get("ONESEG"):
            segs = segs[:1]
        if _os.environ.get("NOSAMPLE"):
            segs[0] = segs[0][:2]
        for i in range(3):
            issue_win_piece(LAYERS[0], i)
        for si, tiles in enumerate(segs):
            for (ti, c0, N, dc, kind, gid) in tiles:
                src = xT.rearrange("(kc p) t -> p kc t", p=128)[:, :, dc:dc + N]
                pr.dma("sp", lambda en, s=src, c0=c0, N=N: en.dma_start(out=XF[:, :, c0:c0 + N], in_=s), "x%d" % ti,
                       writes=[("XF", ti, oc) for oc in range(8)])
                act(XB[:, :, c0:c0 + N], XF[:, :, c0:c0 + N], AF.Copy, [("XF", ti, oc) for oc in range(8)], [("XB", ti)])
            for li, l in enumerate(LAYERS):
                issue_wout(l)
                for tix, (ti, c0, N, dc, kind, gid) in enumerate(tiles):
                    lastp = (gid == 3)
                    defer_flag[0] = (tix == len(tiles) - 1)
                    if kind == "s":
                        pr.barrier()
                    if l % 2 == 0:
                        if kind == "p":
                            even_prompt_tile(l, ti, c0, lastp)
                        else:
                            even_sample_tile(l, ti, c0)
                    else:
                        if kind == "p":
                            odd_prompt_tile(l, ti, c0, lastp)
                        else:
                            odd_sample_tile(l, ti, c0)
                    defer_flag[0] = False
                    if kind == "s" and tix != len(tiles) - 1:
                        pr.barrier()
                if li + 1 < len(LAYERS):
                    nxt = LAYERS[li + 1]
                elif si + 1 < len(segs):
                    nxt = LAYERS[0]
                else:
                    nxt = None
                ffn_phase(l, tiles, nxt, li == len(LAYERS) - 1)
        pr.finish()
        with nc.Block() as block:
            pr.emit(block)
    return nc


def _fm(a, nch):
    lead = a.shape[:-1]
    a = a.reshape(lead + (nch, 128))
    nd = a.ndim
    return np.ascontiguousarray(np.transpose(a, (nd - 1,) + tuple(range(nd - 2)) + (nd - 2,)))


def _pack_pv(inp):
    pv = np.zeros((128, NPV), np.float32)
    pv[:, PV_LNMW:PV_LNMW + 32] = _fm(inp["ln_mix_w"], 8).reshape(128, 32)
    pv[:, PV_LNMB:PV_LNMB + 32] = _fm(inp["ln_mix_b"], 8).reshape(128, 32)
    pv[:, PV_LNFW:PV_LNFW + 32] = _fm(inp["ln_ffn_w"], 8).reshape(128, 32)
    pv[:, PV_LNFB:PV_LNFB + 32] = _fm(inp["ln_ffn_b"], 8).reshape(128, 32)
    dw = _fm(inp["conf_dw_w"], 4)
    pv[:, PV_DWW:PV_DWW + 248] = np.transpose(dw, (0, 1, 3, 2)).reshape(128, 248)
    pv[:, PV_DWB:PV_DWB + 8] = _fm(inp["conf_dw_b"], 4).reshape(128, 8)
    pv[:, PV_CLW:PV_CLW + 8] = _fm(inp["conf_ln_w"], 4).reshape(128, 8)
    pv[:, PV_CLB:PV_CLB + 8] = _fm(inp["conf_ln_b"], 4).reshape(128, 8)
    scw = _fm(inp["sc_conv_w"], 8)
    pv[:, PV_SCW:PV_SCW + 48] = np.transpose(scw, (0, 1, 3, 2)).reshape(128, 48)
    pv[:, PV_LBL:PV_LBL + 8] = _fm(inp["hgrn_lb_logits"], 4).reshape(128, 8)
    pv[:, PV_GNW:PV_GNW + 2] = np.ascontiguousarray(inp["hgrn_gnorm_w"].T)
    return pv


def _consts():
    cf = np.zeros((128, NCST), np.float32)
    cb = np.zeros((128, NCST), np.float32)
    cf[:, C_ID:C_ID + 128] = np.eye(128, dtype=np.float32)
    cb[:, C_ID:C_ID + 128] = np.eye(128, dtype=np.float32)
    p = np.arange(128)[:, None] % 64
    t = np.arange(512)[None, :] % 64
    cb[:, C_TRI:C_TRI + 512] = (t >= p).astype(np.float32)
    cf[:, C_SM:C_SM + 512] = ((np.arange(512) % 64) != 0).astype(np.float32)[None, :]
    cf[:, C_ONE:C_ONE + 128] = 1.0
    cb[:, C_ONE:C_ONE + 128] = 1.0
    return cf, cb


_NC_CACHE = {}


def kernel(NL=DEPTH, **inp):
    inp = {k: np.asarray(v) for k, v in inp.items()}
    if NL not in _NC_CACHE:
        _NC_CACHE[NL] = build(NL)
    nc = _NC_CACHE[NL]
    pv = _pack_pv(inp)
    cstf, cstb = _consts()
    in_maps = []
    for c in range(NCORES):
        bs = slice(c * NB, (c + 1) * NB)
        xT = np.empty((D, NTOK), np.float32)
        xT[:, :2048] = inp["x_prompt"][c].T
        xT[:, 2048:] = inp["x_sample"][bs, 0, :].T
        in_maps.append({
            "xT": xT,
            "s_hgrn": np.ascontiguousarray(np.transpose(inp["state_hgrn"][:, bs], (0, 3, 1, 2, 4))),
            "s_conf": np.ascontiguousarray(np.transpose(inp["state_conf"][:, bs], (0, 3, 1, 2))),
            "s_sconv": np.ascontiguousarray(np.transpose(inp["state_sconv"][:, bs], (0, 3, 1, 2))),
            "w_in_even": inp["w_in_even"], "w_out_even": inp["w_out_even"],
            "sc_w_in": inp["sc_w_in"], "sc_w_out": inp["sc_w_out"],
            "ffn_w1": inp["ffn_w1"], "ffn_w3": inp["ffn_w3"], "ffn_w2": inp["ffn_w2"],
            "pv": pv, "cstf": cstf, "cstb": cstb,
        })
    res = run_bass_kernel_spmd(nc, in_maps, core_ids=list(range(NCORES)))
    R = res.results
    y_prompt = np.stack([R[c]["yT"][:, :2048].T for c in range(NCORES)])
    y_sample = np.concatenate([R[c]["yT"][:, 2048:].T for c in range(NCORES)])[:, None, :]
    h_p = np.stack([R[c]["o_hgrn_p"] for c in range(NCORES)], axis=1)
    c_p = np.stack([np.transpose(R[c]["o_conf_p"], (0, 2, 1)) for c in range(NCORES)], axis=1)
    s_p = np.stack([np.transpose(R[c]["o_sconv_p"], (0, 2, 1)) for c in range(NCORES)], axis=1)
    h_s = np.concatenate([np.transpose(R[c]["o_hgrn_s"], (0, 2, 3, 1, 4)) for c in range(NCORES)], axis=1)
    c_s = np.concatenate([np.transpose(R[c]["o_conf_s"], (0, 2, 3, 1)) for c in range(NCORES)], axis=1)
    s_s = np.concatenate([np.transpose(R[c]["o_sconv_s"], (0, 2, 3, 1)) for c in range(NCORES)], axis=1)
    f = lambda a: np.ascontiguousarray(a, dtype=np.float32)
    return (f(y_prompt), f(y_sample), f(h_p), f(c_p), f(s_p), f(h_s), f(c_s), f(s_s))
```

```python
import math
from contextlib import ExitStack
import numpy as np
import concourse.bass as bass
import concourse.mybir as mybir
from concourse.bass_utils import run_bass_kernel_spmd

F32 = mybir.dt.float32
BF16 = mybir.dt.bfloat16
AF = mybir.ActivationFunctionType
ALU = mybir.AluOpType
AX = mybir.AxisListType

NCORES = 8
D = 1024
KC = 8
DFF = 2816
NJ = 22
NJG = 11
EIN = 3072
DEPTH = 4
ALPHA = (2 * DEPTH) ** 0.25
LN_EPS = 1e-5
RMS_EPS = 1e-6
QSCALE = 128 ** -0.5
SW = 1040
NTOK = 2064
NB = 16

PV_LNMW, PV_LNMB, PV_LNFW, PV_LNFB = 0, 32, 64, 96
PV_DWW = 128
PV_DWB = 376
PV_CLW = 384
PV_CLB = 392
PV_SCW = 400
PV_LBL = 448
PV_GNW = 456
NPV = 460
C_ID = 0
C_TRI = 128
C_SM = 128
C_ONE = 640
NCST = 768


class Prog:
    ENG = ("pe", "act", "dve", "pool", "sp")

    def __init__(self, nc, es):
        self.nc = nc
        self.es = es
        self.q = {e: [] for e in self.ENG}
        self.semh = {e: es.enter_context(nc.semaphore("s_" + e)) for e in self.ENG}
        self.cnt = {e: 0 for e in self.ENG}
        self.seen = {e: {} for e in self.ENG}
        self.lastw = {}
        self.readers = {}
        self.dtot = {}
        self.dbar = {}

    def _deps(self, eng, reads, writes):
        deps = []
        for b in reads:
            if b in self.lastw:
                deps.append(self.lastw[b])
        for b in writes:
            if b in self.lastw:
                deps.append(self.lastw[b])
            deps.extend(self.readers.get(b, ()))
        out = []
        for (sk, val) in deps:
            if sk == "pe" and eng == "pe":
                continue
            if self.seen[eng].get(sk, 0) >= val:
                continue
            self.seen[eng][sk] = val
            out.append((sk, val))
        return out

    def _book(self, me, reads, writes):
        for b in writes:
            self.lastw[b] = me
            self.readers[b] = []
        for b in reads:
            self.readers.setdefault(b, []).append(me)

    def op(self, eng, fn, reads=(), writes=(), signal=True):
        waits = self._deps(eng, reads, writes)
        if signal:
            self.cnt[eng] += 1
            me = (eng, self.cnt[eng])
        else:
            me = (eng, self.cnt[eng] + 1)
        self.q[eng].append((waits, fn, "sig" if signal else None))
        self._book(me, reads, writes)

    def dma(self, eng, fn, sem, reads=(), writes=(), bar=False):
        waits = self._deps(eng, reads, writes)
        key = "d:" + sem
        if key not in self.semh:
            self.semh[key] = self.es.enter_context(self.nc.semaphore("d_" + sem))
            self.dtot[key] = 0
        self.dtot[key] += 16
        self.dbar[key] = bar
        me = (key, self.dtot[key])
        if eng == "sp":
            hist = self.__dict__.setdefault("sp_hist", [])
            if len(hist) >= 2:
                pk, pv_ = hist[-2]
                if pk != key and self.seen[eng].get(pk, 0) < pv_:
                    self.seen[eng][pk] = pv_
                    waits.append((pk, pv_))
            hist.append(me)
        self.q[eng].append((waits, fn, key))
        self._book(me, reads, writes)

    def barrier(self):
        snap = dict(self.cnt)
        dsn = {k: v for k, v in self.dtot.items() if self.dbar.get(k)}
        for e in self.ENG:
            waits = []
            for e2 in self.ENG:
                if e2 == e or snap[e2] == 0:
                    continue
                if self.seen[e].get(e2, 0) < snap[e2]:
                    self.seen[e][e2] = snap[e2]
                    waits.append((e2, snap[e2]))
            for k, v in dsn.items():
                if self.seen[e].get(k, 0) < v:
                    self.seen[e][k] = v
                    waits.append((k, v))
            if waits:
                self.q[e].append((waits, None, None))
        for b in list(self.lastw.keys()):
            if not self.lastw[b][0].startswith("d:"):
                del self.lastw[b]
        for b in list(self.readers.keys()):
            self.readers[b] = [r for r in self.readers[b] if r[0].startswith("d:") and not self.dbar.get(r[0])]

    def finish(self):
        waits = [(k, v) for k, v in self.dtot.items()]
        waits += [(e, self.cnt[e]) for e in self.ENG if e != "sp" and self.cnt[e] > 0]
        self.q["sp"].append((waits, None, None))

    def emit(self, block):
        for e, attr in (("pe", "tensor"), ("act", "scalar"), ("dve", "vector"), ("pool", "gpsimd"), ("sp", "sync")):
            def body(engobj, e=e):
                for waits, fn, sig in self.q[e]:
                    for (sk, val) in waits:
                        engobj.wait_ge(self.semh[sk], val)
                    if fn is None:
                        continue
                    ins = fn(engobj)
                    if sig == "sig":
                        ins.then_inc(self.semh[e], 1)
                    elif sig is not None:
                        ins.then_inc(self.semh[sig], 16)
            getattr(block, attr)(body)


def build(NL=DEPTH):
    nc = bass.Bass("TRN2", target_bir_lowering=False)

    def din(name, shape):
        return nc.dram_tensor(name, shape, F32, kind="ExternalInput").ap()

    def dout(name, shape):
        return nc.dram_tensor(name, shape, F32, kind="ExternalOutput").ap()

    xT = din("xT", [D, NTOK])
    s_hgrn = din("s_hgrn", [2, 128, NB, 4, 128])
    s_conf = din("s_conf", [2, 512, NB, 30])
    s_sconv = din("s_sconv", [2, D, NB, 2])
    w_in_even = din("w_in_even", [2, D, EIN])
    w_out_even = din("w_out_even", [2, D, D])
    sc_w_in = din("sc_w_in", [2, D, EIN])
    sc_w_out = din("sc_w_out", [2, D, D])
    ffn_w1 = din("ffn_w1", [DEPTH, D, DFF])
    ffn_w3 = din("ffn_w3", [DEPTH, D, DFF])
    ffn_w2 = din("ffn_w2", [DEPTH, DFF, D])
    pv_d = din("pv", [128, NPV])
    cstf_d = din("cstf", [128, NCST])
    cstb_d = din("cstb", [128, NCST])

    yT = dout("yT", [D, NTOK])
    o_hgrn_p = dout("o_hgrn_p", [2, 4, 128, 128])
    o_conf_p = dout("o_conf_p", [2, 512, 30])
    o_sconv_p = dout("o_sconv_p", [2, D, 2])
    o_hgrn_s = dout("o_hgrn_s", [2, 128, NB, 4, 128])
    o_conf_s = dout("o_conf_s", [2, 512, NB, 30])
    o_sconv_s = dout("o_sconv_s", [2, D, NB, 2])

    es = ExitStack()
    with es:
        def sb(name, shape, dt=F32):
            return es.enter_context(nc.sbuf_tensor(name, shape, dt))

        XF = sb("XF", [128, KC, SW])
        XB = sb("XB", [128, KC, SW], BF16)
        PV = sb("PV", [128, NPV])
        CF = sb("CF", [128, NCST])
        CB = sb("CB", [128, NCST], BF16)
        LBV = sb("LBV", [128, 3, 2, 4])
        ZST = sb("ZST", [128, 2, 4, 128])
        SBF = sb("SBF", [128, 4, 128], BF16)
        EBA = sb("EBA", [128, 2, 4, 9])
        UT = sb("UT", [128, 2, 4, 30])
        ZT = sb("ZT", [128, 2, 8, 2])
        SFIN = sb("SFIN", [128, 4, 128])
        WIN = sb("WIN", [128, KC, EIN], BF16)
        W2O = sb("W2O", [128, 8192], BF16)
        SCRB = 74240
        SCR = sb("SCR", [128, SCRB // 4])
        SCRH = SCR.bitcast(BF16)
        PS = es.enter_context(nc.psum_tensor("PS", [128, 4096], F32))
        PSH = PS.bitcast(BF16)

        pr = Prog(nc, es)

        IDB = CB[:, C_ID:C_ID + 128]
        ONEB = CB[:, C_ONE:C_ONE + 128]
        TRI = CB[:, C_TRI:C_TRI + 512]
        SMASK = CF[:, C_SM:C_SM + 512]
        IDF = CF[:, C_ID:C_ID + 128]
        ONEF = CF[:, C_ONE:C_ONE + 128]

        off = [0]

        def carve(nbytes):
            o = off[0]
            off[0] += (nbytes + 63) // 64 * 64
            return o

        def vf(o, n):
            return SCR[:, o // 4:o // 4 + n]

        def vh(o, n):
            return SCRH[:, o // 2:o // 2 + n]

        T = [vf(carve(2048), 512) for _ in range(6)]
        o_alias = off[0]
        ACC = vf(carve(8192), 2048).rearrange("p (a b) -> p a b", a=4)
        QT = vh(carve(4096), 2048).rearrange("p (a b) -> p a b", a=4)
        KT = vh(carve(4096), 2048).rearrange("p (a b) -> p a b", a=4)
        VT = vh(carve(4096), 2048).rearrange("p (a b) -> p a b", a=4)
        SG = vh(carve(4096), 2048).rearrange("p (a b) -> p a b", a=4)
        KTT = vh(carve(4096), 2048)
        SC = vh(carve(2048), 1024)
        o_alias_end = off[0]
        M = vh(carve(8192), 4096).rearrange("p (a b) -> p a b", a=8)
        o_u = carve(8704)
        UREG = vf(o_u, 4 * 544)
        U = vh(o_u, 4 * 542).rearrange("p (a b) -> p a b", a=4)
        DG = [vh(o_u + 4352 + i * 256, 128) for i in range(16)]
        o_ln = off[0]
        RBT = [vh(carve(1024), 512) for _ in range(2)]
        R2T = [vh(carve(1024), 512) for _ in range(2)]
        MU = vf(carve(2048), 512)
        RS = vf(carve(2048), 512)
        MSQ = vf(carve(2048), 512)
        VSB = vf(carve(2048), 512)
        VMB = vf(carve(2048), 512)
        assert off[0] <= SCRB, off[0]
        off[0] = o_alias
        SRF = [vf(carve(8192), 2048) for _ in range(2)]
        SR = [x.rearrange("p (q a b) -> p q a b", q=4, a=4) for x in SRF]
        UHF = vf(carve(7680), 4 * 16 * 30)
        UH = UHF.rearrange("p (a b c) -> p a b c", a=4, b=16)
        TMPC = vf(carve(1920), 16 * 30).rearrange("p (b c) -> p b c", b=16)
        KV = vf(carve(2048), 512).rearrange("p (a b) -> p a b", a=4)
        PZS = vf(carve(1536), 384).rearrange("p (a b) -> p a b", a=24)
        POS = vf(carve(256), 64).rearrange("p (a b) -> p a b", a=4)
        assert off[0] <= o_alias_end, (off[0], o_alias_end)
        off[0] = o_alias
        ZB = vf(carve(8 * 514 * 4), 8 * 514).rearrange("p (a b) -> p a b", a=8)
        assert off[0] <= o_alias_end
        off[0] = 0
        H = vh(carve(NJG * SW * 2), NJG * SW).rearrange("p (a b) -> p a b", a=NJG)
        W1S = [vh(carve(8192), 4096).rearrange("p (a b) -> p a b", a=8) for _ in range(2)]
        W3S = [vh(carve(8192), 4096).rearrange("p (a b) -> p a b", a=8) for _ in range(2)]
        FT = [vf(carve(2048), 512) for _ in range(2)]
        o_ffn_end = off[0]
        assert o_ffn_end <= o_ln
        WOUT = W2O[:, 0:8192].rearrange("p (a b) -> p a b", a=8)
        W2S = [W2O[:, i * 2816:(i + 1) * 2816].rearrange("p (a b) -> p a b", a=NJG) for i in range(2)]

        def bank(b, n=512, p0=0, p1=128):
            return PS[p0:p1, b * 512:b * 512 + n]

        def act(out, in_, func, reads, writes, scale=None, bias=None):
            kw = {}
            if scale is not None:
                kw["scale"] = scale
            if bias is not None:
                kw["bias"] = bias
            pr.op("act", lambda e: e.activation(out=out, in_=in_, func=func, **kw), reads, writes)

        def tt(out, a, b, op, reads, writes):
            pr.op("dve", lambda e: e.tensor_tensor(out=out, in0=a, in1=b, op=op), reads, writes)

        def ts(out, a, s1, s2, op0, op1, reads, writes):
            pr.op("dve", lambda e: e.tensor_scalar(out=out, in0=a, scalar1=s1, scalar2=s2, op0=op0, op1=op1),
                  reads, writes)

        def ts1(out, a, s1, op0, reads, writes):
            pr.op("dve", lambda e: e.tensor_scalar(out=out, in0=a, scalar1=s1, scalar2=None, op0=op0), reads, writes)

        def stt(out, a, s, b, op0, op1, reads, writes):
            pr.op("dve", lambda e: e.scalar_tensor_tensor(out=out, in0=a, scalar=s, in1=b, op0=op0, op1=op1),
                  reads, writes)

        def mm(out, lhsT, rhs, start, stop, reads, writes, signal):
            pr.op("pe", lambda e: e.matmul(out, lhsT=lhsT, rhs=rhs, start=start, stop=stop), reads, writes, signal)

        def mm_group(out, pairs, reads, wkey):
            n = len(pairs)
            for i, (l, r) in enumerate(pairs):
                mm(out, l, r, i == 0, i == n - 1, reads, [wkey], i == n - 1)

        cst_biases = {}

        def fbias(val):
            if val not in cst_biases:
                cst_biases[val] = len(cst_biases)
            return BIASC[:, cst_biases[val]:cst_biases[val] + 1]

        BIASC = sb("BIASC", [128, 4])

        pr.dma("sp", lambda e: e.dma_start(out=PV[:, :], in_=pv_d[:, :]), "pv", writes=["PV"])
        pr.dma("sp", lambda e: e.dma_start(out=CF[:, :], in_=cstf_d[:, :]), "cf", writes=["CF"])
        pr.dma("pool", lambda e: e.dma_start(out=CB[:, :], in_=cstb_d[:, :]), "cb", writes=["CB"])
        pr.op("dve", lambda e: e.memset(ZST[:, :, :, :], 0.0), writes=[("Z%d" % e_, h_) for e_ in range(2) for h_ in range(4)])
        pr.op("dve", lambda e: e.memset(EBA[:, :, :, :], 1.0), writes=["EBA0", "EBA1"])
        pr.op("dve", lambda e: e.memset(UT[:, :, :, :], 0.0), writes=["UT0", "UT1"])
        pr.op("dve", lambda e: e.memset(ZT[:, :, :, :], 0.0), writes=["ZT0", "ZT1"])
        pr.op("dve", lambda e: e.memset(SBF[:, :, :], 0.0), writes=[("SBF", h_) for h_ in range(4)])
        pr.op("dve", lambda e: e.memset(BIASC[:, 0:1], LN_EPS), writes=["BIASC"])
        pr.op("dve", lambda e: e.memset(BIASC[:, 1:2], RMS_EPS), writes=["BIASC"])
        cst_biases[LN_EPS] = 0
        cst_biases[RMS_EPS] = 1
        pr.op("dve", lambda e: e.memset(LBV[:, 0, 0, :], 0.0), writes=["LBV"])
        tt(LBV[:, 0, 1, :], PV[:, PV_LBL + 4:PV_LBL + 8], PV[:, PV_LBL:PV_LBL + 4], ALU.subtract, ["PV"], ["LBV"])
        act(LBV[:, 0, 1, :], LBV[:, 0, 1, :], AF.Sigmoid, ["LBV"], ["LBV"])
        ts(LBV[:, 1, :, :], LBV[:, 0, :, :], -1.0, 1.0, ALU.mult, ALU.add, ["LBV"], ["LBV"])
        ts1(LBV[:, 2, :, :], LBV[:, 1, :, :], -1.0, ALU.mult, ["LBV"], ["LBV"])

        win_src = []
        wout_src = []
        for l in range(DEPTH):
            if l % 2 == 0:
                win_src.append(w_in_even[l // 2])
                wout_src.append(w_out_even[l // 2])
            else:
                win_src.append(sc_w_in[l // 2])
                wout_src.append(sc_w_out[l // 2])

        def issue_win_piece(l, i):
            src = win_src[l].rearrange("(kc p) n -> p kc n", p=128)[:, :, i * 1024:(i + 1) * 1024]
            dst = WIN[:, :, i * 1024:(i + 1) * 1024]
            pr.dma("pool", lambda e: e.dma_start(out=dst, in_=src), "win%d" % i, writes=[("WIN", i)])

        def issue_wout(l):
            src = wout_src[l].rearrange("(kc p) n -> p kc n", p=128)
            pr.dma("pool", lambda e, s=src: e.dma_start(out=WOUT[:, :, :], in_=s), "wout",
                   writes=[("W2O", 0), ("W2O", 1)])

        def ln_stats(srcs, N, inv_n, eps, bm, bv):
            n = len(srcs)
            for i, (ap, key) in enumerate(srcs):
                rb = RBT[i % 2][:, 0:N]
                r2 = R2T[i % 2][:, 0:N]
                act(rb, ap, AF.Copy, [key], [("RBT", i % 2)])
                act(r2, ap, AF.Square, [key], [("R2T", i % 2)])
                mm(bank(bm, N), ONEB, rb, i == 0, i == n - 1, [("RBT", i % 2), "CB"], [("ps", bm)], True)
                mm(bank(bv, N), ONEB, r2, i == 0, i == n - 1, [("R2T", i % 2), "CB"], [("ps", bv)], True)
            ts1(MU[:, 0:N], bank(bm, N), inv_n, ALU.mult, [("ps", bm)], ["MU"])
            tt(MSQ[:, 0:N], MU[:, 0:N], MU[:, 0:N], ALU.mult, ["MU"], ["MSQ"])
            stt(MSQ[:, 0:N], bank(bv, N), inv_n, MSQ[:, 0:N], ALU.mult, ALU.subtract, [("ps", bv), "MSQ"], ["MSQ"])
            act(MSQ[:, 0:N], MSQ[:, 0:N], AF.Ln, ["MSQ", "BIASC"], ["MSQ"], bias=fbias(eps))
            act(RS[:, 0:N], MSQ[:, 0:N], AF.Exp, ["MSQ"], ["RS"], scale=-0.5)

        def deepnorm_ln(ti, c0, N, wcol, bcol, out_cols):
            srcs = [(XF[:, oc, c0:c0 + N], ("XF", ti, oc)) for oc in range(8)]
            ln_stats(srcs, N, 1.0 / D, LN_EPS, 6, 7)
            for oc in range(8):
                x = XF[:, oc, c0:c0 + N]
                k = ("XF", ti, oc)
                tt(x, x, MU[:, 0:N], ALU.subtract, [k, "MU"], [k])
                tt(x, x, RS[:, 0:N], ALU.mult, [k, "RS"], [k])
                act(x, x, AF.Identity, [k, "PV"], [k], scale=PV[:, wcol + oc:wcol + oc + 1],
                    bias=PV[:, bcol + oc:bcol + oc + 1])
                act(XB[:, oc, c0:c0 + N], x, AF.Copy, [k], [("XB", ti)])
            if out_cols is not None:
                dst = yT.rearrange("(kc p) t -> p kc t", p=128)[:, :, out_cols:out_cols + N]
                pr.dma("sp", lambda e: e.dma_start(out=dst, in_=XF[:, :, c0:c0 + N]), "y%d" % ti,
                       reads=[("XF", ti, oc) for oc in range(8)])

        pending_ln = []

        def out_proj_ln(l, ti, c0, N):
            for oc in range(8):
                b = oc % 4
                mm_group(bank(b, N), [(WOUT[:, kc, oc * 128:(oc + 1) * 128], M[:, kc, 0:N]) for kc in range(8)],
                         [("W2O", 0), "M"], ("ps", b))
                x = XF[:, oc, c0:c0 + N]
                stt(x, x, ALPHA, bank(b, N), ALU.mult, ALU.add, [("XF", ti, oc), ("ps", b)], [("XF", ti, oc)])
            if defer_flag[0]:
                pending_ln.append(lambda: deepnorm_ln(ti, c0, N, PV_LNMW + l * 8, PV_LNMB + l * 8, None))
            else:
                deepnorm_ln(ti, c0, N, PV_LNMW + l * 8, PV_LNMB + l * 8, None)

        defer_flag = [False]
        rot = [0]

        def nb(nbanks=4, base=0):
            rot[0] = (rot[0] + 1) % nbanks
            return base + rot[0]

        def inproj_fm(oc, c0, N, ti, b, col=0):
            out = PS[:, b * 512 + col:b * 512 + col + N]
            mm_group(out, [(WIN[:, kc, oc * 128:(oc + 1) * 128], XB[:, kc, c0:c0 + N]) for kc in range(8)],
                     [("WIN", oc // 8), ("XB", ti)], ("ps", b))

        def conf_ln_silu(e, N, accv):
            srcs = [(accv[:, cc, 0:N], "ACC") for cc in range(4)]
            ln_stats(srcs, N, 1.0 / 512, LN_EPS, 6, 7)
            for cc in range(4):
                a = accv[:, cc, 0:N]
                tt(a, a, MU[:, 0:N], ALU.subtract, ["ACC", "MU"], ["ACC"])
                tt(a, a, RS[:, 0:N], ALU.mult, ["ACC", "RS"], ["ACC"])
                act(M[:, 4 + cc, 0:N], a, AF.Silu, ["ACC", "PV"], ["M"],
                    scale=PV[:, PV_CLW + e * 4 + cc:PV_CLW + e * 4 + cc + 1],
                    bias=PV[:, PV_CLB + e * 4 + cc:PV_CLB + e * 4 + cc + 1])

        def rms_gate(e, h, N, o_ps, o_key, sg_ap, sg_key, sbank):
            r2 = R2T[h % 2][:, 0:N]
            act(r2, o_ps, AF.Square, [o_key], [("R2T", h % 2)])
            mm(bank(sbank, N), ONEB, r2, True, True, [("R2T", h % 2), "CB"], [("ps", sbank)], True)
            act(RS[:, 0:N], bank(sbank, N), AF.Ln, [("ps", sbank), "BIASC"], ["RS"], scale=1.0 / 128, bias=fbias(RMS_EPS))
            act(RS[:, 0:N], RS[:, 0:N], AF.Exp, ["RS"], ["RS"], scale=-0.5)
            tt(MU[:, 0:N], o_ps, RS[:, 0:N], ALU.mult, [o_key, "RS"], ["MU"])
            stt(M[:, h, 0:N], MU[:, 0:N], PV[:, PV_GNW + e:PV_GNW + e + 1], sg_ap, ALU.mult, ALU.mult,
                ["MU", "PV", sg_key], ["M"])

        def even_prompt_tile(l, ti, c0, last_prompt):
            e = l // 2
            N = 512
            lb = lambda h: LBV[:, 0, e, h:h + 1]
            omlb = lambda h: LBV[:, 1, e, h:h + 1]
            nomlb = lambda h: LBV[:, 2, e, h:h + 1]
            act(U[:, :, 0:30], UT[:, e, :, :], AF.Copy, ["UT%d" % e], ["U"])
            for cc in range(4):
                ba, bg = 2 * (cc % 2), 2 * (cc % 2) + 1
                inproj_fm(16 + cc, c0, N, ti, ba)
                inproj_fm(20 + cc, c0, N, ti, bg)
                tmp = T[cc % 2][:, 0:N]
                act(tmp, bank(bg, N), AF.Sigmoid, [("ps", bg)], [("T", cc % 2)])
                tt(U[:, cc, 30:30 + N], bank(ba, N), tmp, ALU.mult, [("ps", ba), ("T", cc % 2)], ["U"])
                tt(UT[:, e, cc, :], bank(ba, N)[:, N - 30:N], tmp[:, N - 30:N], ALU.mult,
                   [("ps", ba), ("T", cc % 2)], ["UT%d" % e])
            wcol = lambda cc, k: PV[:, PV_DWW + (e * 4 + cc) * 31 + k:PV_DWW + (e * 4 + cc) * 31 + k + 1]

            dgi = [0]

            def conv_taps(k0, k1):
                for k in range(k0, k1):
                    for cc in range(4):
                        i = dgi[0] % 16
                        dgi[0] += 1
                        ts1(DG[i], IDB, wcol(cc, k), ALU.mult, ["CB", "PV"], [("DG", i)])
                        mm(bank(4 + cc, N), DG[i], U[:, cc, k:k + N], k == 0, k == 30, [("DG", i), "U"],
                           [("ps", 4 + cc)], True)
            for h in range(4):
                bq, bf_, bg = nb(), nb(), nb()
                inproj_fm(0 + h, c0, N, ti, bq)
                inproj_fm(4 + h, c0, N, ti, bf_)
                inproj_fm(12 + h, c0, N, ti, bg)
                conv_taps(8 * h, min(8 * h + 8, 31) if h < 3 else 31)
                t1, t2, t3, t4, t5, t6 = [T[i][:, 0:N] for i in range(6)]
                act(t1, bank(bq, N), AF.Sigmoid, [("ps", bq)], [("T", 0)])
                tt(t1, bank(bq, N), t1, ALU.mult, [("ps", bq), ("T", 0)], [("T", 0)])
                act(t2, bank(bf_, N), AF.Sigmoid, [("ps", bf_)], [("T", 1)])
                ts(t3, t2, nomlb(h), omlb(h), ALU.mult, ALU.add, [("T", 1), "LBV"], [("T", 2)])
                act(t2, t2, AF.Ln, [("T", 1), "LBV"], [("T", 1)], scale=omlb(h), bias=lb(h))
                pr.op("dve", lambda en, t4=t4, t2=t2: en.tensor_tensor_scan(
                    out=t4, data0=SMASK[:, 0:N], data1=t2, initial=0.0, op0=ALU.mult, op1=ALU.add),
                    [("T", 1), "CF"], [("T", 3)])
                act(t5, t4, AF.Exp, [("T", 3)], [("T", 4)])
                act(t6, t4, AF.Exp, [("T", 3)], [("T", 5)], scale=-1.0)
                stt(QT[:, h, :], t1, QSCALE, t5, ALU.mult, ALU.mult, [("T", 0), ("T", 4)], ["QT"])
                tt(KT[:, h, :], t3, t6, ALU.mult, [("T", 2), ("T", 5)], ["KT"])
                act(EBA[:, e, h, 1:9], T[4][:, 63:512:64], AF.Copy, [("T", 4)], ["EBA%d" % e])
                act(t1, bank(bg, N), AF.Sigmoid, [("ps", bg)], [("T", 0)])
                tt(SG[:, h, :], bank(bg, N), t1, ALU.mult, [("ps", bg), ("T", 0)], ["SG"])
            for cc in range(4):
                act(ACC[:, cc, 0:N], bank(4 + cc, N), AF.Identity, [("ps", 4 + cc), "PV"], ["ACC"],
                    bias=PV[:, PV_DWB + e * 4 + cc:PV_DWB + e * 4 + cc + 1])
            for sub in range(4):
                b = nb()
                mm_group(bank(b), [(XB[:, kc, c0 + sub * 128:c0 + (sub + 1) * 128], WIN[:, kc, 1024:1536])
                                   for kc in range(8)], [("WIN", 1), ("XB", ti)], ("ps", b))
                act(VT[:, sub, :], bank(b), AF.Copy, [("ps", b)], ["VT"])
            for h in range(4):
                bs = 4 + h // 2
                bt = 6 + h // 2
                for c in range(8):
                    hp = (c % 2) * 64
                    col = bs * 512 + (h % 2) * 256 + (c // 2) * 64
                    mm(PS[hp:hp + 64, col:col + 64], KT[:, h, c * 64:(c + 1) * 64], QT[:, h, c * 64:(c + 1) * 64],
                       True, True, ["KT", "QT"], [("ps", bs)], c == 7)
                for c in range(8):
                    hp = (c % 2) * 64
                    col = bt * 1024 + (h % 2) * 512 + (c // 2) * 128
                    pr.op("pe", lambda en, hp=hp, col=col, h=h, c=c: en.transpose(
                        out=PSH[hp:hp + 64, col:col + 128], in_=KT[:, h, c * 64:(c + 1) * 64], identity=IDB),
                        ["KT", "CB"], [("ps", bt)], c == 7)
            for i in range(2):
                tt(SC[:, i * 512:(i + 1) * 512], bank(4 + i), TRI, ALU.mult, [("ps", 4 + i), "CB"], ["SC"])
                act(KTT[:, i * 1024:(i + 1) * 1024], PSH[:, (6 + i) * 1024:(7 + i) * 1024], AF.Copy,
                    [("ps", 6 + i)], ["KTT"])
            conf_ln_silu(e, N, ACC)
            if last_prompt:
                dstc = o_conf_p[e].rearrange("(cc p) k -> p cc k", p=128)
                pr.dma("sp", lambda en: en.dma_start(out=dstc, in_=UT[:, e, :, :]), "ocp", reads=["UT%d" % e])
            zk = "Z%d" % e
            for h in range(4):
                act(SBF[:, h, :], ZST[:, e, h, :], AF.Identity, [(zk, h), "EBA%d" % e], [("SBF", h)],
                    scale=EBA[:, e, h, 0:1])
            for c in range(8):
                hp = (c % 2) * 64
                sub = c // 2
                for h in range(4):
                    o_out = PS[:, h * 512 + c * 64:h * 512 + (c + 1) * 64]
                    mm(o_out, VT[hp:hp + 64, sub, h * 128:(h + 1) * 128],
                       SC[hp:hp + 64, h * 256 + sub * 64:h * 256 + (sub + 1) * 64], True, False,
                       ["VT", "SC"], [("ps", h)], False)
                    bu = 4 + h
                    mm(PS[:, bu * 512:bu * 512 + 128],
                       KTT[hp:hp + 64, (h * 4 + sub) * 128:(h * 4 + sub + 1) * 128],
                       VT[hp:hp + 64, sub, h * 128:(h + 1) * 128], True, True, ["KTT", "VT"], [("ps", bu)], True)
                for h in range(4):
                    o_out = PS[:, h * 512 + c * 64:h * 512 + (c + 1) * 64]
                    mm(o_out, SBF[:, h, :], QT[:, h, c * 64:(c + 1) * 64], False, True,
                       [("SBF", h), "QT"], [("ps", h)], True)
                for h in range(4):
                    bu = 4 + h
                    z = ZST[:, e, h, :]
                    stt(z, z, EBA[:, e, h, c:c + 1], PS[:, bu * 512:bu * 512 + 128],
                        ALU.mult, ALU.add, [(zk, h), "EBA%d" % e, ("ps", bu)], [(zk, h)])
                    act(SBF[:, h, :], z, AF.Identity, [(zk, h), "EBA%d" % e], [("SBF", h)], scale=EBA[:, e, h, c + 1:c + 2])
            if last_prompt:
                for h in range(4):
                    ts1(SFIN[:, h, :], ZST[:, e, h, :], EBA[:, e, h, 8:9], ALU.mult, [(zk, h), "EBA%d" % e], ["SFIN"])
                dst = o_hgrn_p[e].rearrange("h k v -> k h v")
                pr.dma("sp", lambda en: en.dma_start(out=dst, in_=SFIN[:, :, :]), "ohp", reads=["SFIN"], writes=[])
            act(EBA[:, e, :, 0], EBA[:, e, :, 8], AF.Copy, ["EBA%d" % e], ["EBA%d" % e])
            for h in range(4):
                rms_gate(e, h, N, bank(h), ("ps", h), SG[:, h, :], "SG", 6 + h % 2)
            out_proj_ln(l, ti, c0, N)

        def even_sample_tile(l, ti, c0):
            e = l // 2
            N = NB
            PZP = PS[:, 0:24 * N].rearrange("p (a b) -> p a b", a=24)
            for oc in list(range(0, 8)) + list(range(12, 24)):
                inproj_fm(oc, c0, N, ti, 0, col=oc * N)
            act(PZS[:, 0:8, :], PZP[:, 0:8, :], AF.Copy, [("ps", 0)], ["PZS"])
            act(PZS[:, 12:24, :], PZP[:, 12:24, :], AF.Copy, [("ps", 0)], ["PZS"])
            PZ = PZS
            mm_group(PS[0:N, 512:1024], [(XB[:, kc, c0:c0 + N], WIN[:, kc, 1024:1536]) for kc in range(8)],
                     [("WIN", 1), ("XB", ti)], ("ps", 1))
            small = lambda i: T[i][:, 0:4 * N].rearrange("p (a b) -> p a b", a=4)
            QS, FS, KK, SGS, TMPS, ACS = [small(i) for i in range(6)]
            VS = VSB[0:N, 0:512]
            VM = VMB[0:N, 0:512]
            act(QS, PZ[:, 0:4, :], AF.Sigmoid, ["PZS"], [("T", 0)])
            stt(QS, PZ[:, 0:4, :], QSCALE, QS, ALU.mult, ALU.mult, ["PZS", ("T", 0)], [("T", 0)])
            act(TMPS, PZ[:, 4:8, :], AF.Sigmoid, ["PZS"], [("T", 4)])
            for h in range(4):
                ts(FS[:, h, :], TMPS[:, h, :], LBV[:, 1, e, h:h + 1], LBV[:, 0, e, h:h + 1], ALU.mult, ALU.add,
                   [("T", 4), "LBV"], [("T", 1)])
            ts(KK, FS, -1.0, 1.0, ALU.mult, ALU.add, [("T", 1)], [("T", 2)])
            act(SGS, PZ[:, 12:16, :], AF.Sigmoid, ["PZS"], [("T", 3)])
            tt(SGS, PZ[:, 12:16, :], SGS, ALU.mult, ["PZS", ("T", 3)], [("T", 3)])
            act(VS, PS[0:N, 512:1024], AF.Copy, [("ps", 1)], ["VSB"])
            OUTUF = UREG[:, 0:4 * 16 * 30]
            OUTU = OUTUF.rearrange("p (a b c) -> p a b c", a=4, b=16)
            UN = T[4][:, 64:64 + 4 * N].rearrange("p (a b) -> p a b", a=4)
            srcu = s_conf[e].rearrange("(cc p) b k -> p cc (b k)", p=128)
            pr.dma("sp", lambda en: en.dma_start(out=UHF.rearrange("p (a b) -> p a b", a=4), in_=srcu), "ust",
                   writes=["UH"], bar=True)
            act(TMPS, PZ[:, 20:24, :], AF.Sigmoid, ["PZS"], [("T", 4)])
            tt(UN, PZ[:, 16:20, :], TMPS, ALU.mult, ["PZS", ("T", 4)], ["UN"])
            for cc in range(4):
                wap = PV[:, PV_DWW + (e * 4 + cc) * 31:PV_DWW + (e * 4 + cc) * 31 + 30]
                wbc = bass.AP(wap.tensor, wap.offset, [list(wap.ap[0]), [0, N], [1, 30]])
                w30 = PV[:, PV_DWW + (e * 4 + cc) * 31 + 30:PV_DWW + (e * 4 + cc) * 31 + 31]
                tt(TMPC, UH[:, cc, :, :], wbc, ALU.mult, ["UH", "PV"], ["TMPC"])
                pr.op("dve", lambda en, cc=cc: en.tensor_reduce(out=ACS[:, cc, :], in_=TMPC, axis=AX.X, op=ALU.add),
                      ["TMPC"], ["ACC"])
                stt(ACS[:, cc, :], UN[:, cc, :], w30, ACS[:, cc, :], ALU.mult, ALU.add, ["UN", "PV", "ACC"], ["ACC"])
                ts1(ACS[:, cc, :], ACS[:, cc, :], PV[:, PV_DWB + e * 4 + cc:PV_DWB + e * 4 + cc + 1], ALU.add,
                    ["ACC", "PV"], ["ACC"])
            act(OUTU[:, :, :, 0:29], UH[:, :, :, 1:30], AF.Copy, ["UH"], ["U"])
            act(OUTU[:, :, :, 29], UN, AF.Copy, ["UN"], ["U"])
            dstu = o_conf_s[e].rearrange("(cc p) b k -> p cc (b k)", p=128)
            pr.dma("sp", lambda en: en.dma_start(out=dstu, in_=OUTUF.rearrange("p (a b) -> p a b", a=4)), "ocs",
                   reads=["U"], bar=True)
            conf_ln_silu(e, N, ACS)
            PO = PS[:, 1024:1024 + 4 * N].rearrange("p (a b) -> p a b", a=4)
            VMS = [VMB[0:N, 0:512], MSQ[0:N, 0:512]]
            VMK = ["VMB", "MSQ"]

            def mk_vm(b):
                ts1(VMS[b % 2], VS, IDF[0:N, b:b + 1], ALU.mult, ["VSB", "CF"], [VMK[b % 2]])

            def mk_vbc(b):
                bx = 3 + (b % 2)
                mm(bank(bx), ONEF[0:N, :], VMS[b % 2], True, True, [VMK[b % 2], "CF"], [("ps", bx)], True)

            mk_vm(0)
            mk_vm(1)
            mk_vbc(0)
            for b in range(NB):
                q, bb = b // 4, b % 4
                sl = q % 2
                if bb == 0:
                    src = s_hgrn[e][:, 4 * q:4 * q + 4, :, :].rearrange("p q a b -> p (q a b)")
                    pr.dma("sp", lambda en, s=src, sl=sl: en.dma_start(out=SRF[sl], in_=s), "sr%d" % sl,
                           writes=[("SR", sl)], bar=True)
                if b + 1 < NB:
                    mk_vbc(b + 1)
                if b + 2 < NB:
                    mk_vm(b + 2)
                bx = 3 + (b % 2)
                for h in range(4):
                    ts1(KV[:, h, :], bank(bx)[:, h * 128:(h + 1) * 128], KK[:, h, b:b + 1], ALU.mult,
                        [("ps", bx), ("T", 2)], [("KV", h)])
                    stt(SR[sl][:, bb, h, :], SR[sl][:, bb, h, :], FS[:, h, b:b + 1], KV[:, h, :], ALU.mult, ALU.add,
                        [("SR", sl), ("T", 1), ("KV", h)], [("SR", sl)])
                for h in range(4):
                    mm(PO[:, h, b:b + 1], SR[sl][:, bb, h, :], QS[:, h, b:b + 1], True, True,
                       [("SR", sl), ("T", 0)], [("ps", 2)], True)
                if bb == 3:
                    dst = o_hgrn_s[e][:, 4 * q:4 * q + 4, :, :].rearrange("p q a b -> p (q a b)")
                    pr.dma("sp", lambda en, d=dst, sl=sl: en.dma_start(out=d, in_=SRF[sl]), "so%d" % sl,
                           reads=[("SR", sl)], bar=True)
            act(POS, PO, AF.Copy, [("ps", 2)], ["POS"])
            for h in range(4):
                rms_gate(e, h, N, POS[:, h, :], "POS", SGS[:, h, :], ("T", 3), 6 + h % 2)
            out_proj_ln(l, ti, c0, N)

        def odd_prompt_tile(l, ti, c0, last_prompt):
            o = l // 2
            N = 512
            act(ZB[:, :, 0:2], ZT[:, o, :, :], AF.Copy, ["ZT%d" % o], ["ZB"])
            for kc in range(8):
                ba, bc, bx = nb(6), nb(6), nb(6)
                inproj_fm(kc, c0, N, ti, ba)
                inproj_fm(8 + kc, c0, N, ti, bc)
                inproj_fm(16 + kc, c0, N, ti, bx)
                A = T[kc % 2][:, 0:N]
                Y = T[2 + kc % 2][:, 0:N]
                act(A, bank(bc, N), AF.Copy, [("ps", bc)], [("T", kc % 2)])
                z = ZB[:, kc, 2:2 + N]
                tt(z, bank(bx, N), A, ALU.mult, [("ps", bx), ("T", kc % 2)], [("ZB", kc)])
                w = lambda k: PV[:, PV_SCW + (o * 8 + kc) * 3 + k:PV_SCW + (o * 8 + kc) * 3 + k + 1]
                yk = ("T", 2 + kc % 2)
                ts1(Y, z, w(2), ALU.mult, [("ZB", kc), "PV"], [yk])
                stt(Y, ZB[:, kc, 1:1 + N], w(1), Y, ALU.mult, ALU.add, [("ZB", kc), "ZB", "PV", yk], [yk])
                stt(Y, ZB[:, kc, 0:N], w(0), Y, ALU.mult, ALU.add, [("ZB", kc), "ZB", "PV", yk], [yk])
                tt(M[:, kc, 0:N], bank(ba, N), Y, ALU.mult, [("ps", ba), yk], ["M"])
            act(ZT[:, o, :, :], ZB[:, :, N:N + 2], AF.Copy, ["ZB"] + [("ZB", kc) for kc in range(8)], ["ZT%d" % o])
            if last_prompt:
                dst = o_sconv_p[o].rearrange("(kc p) k -> p kc k", p=128)
                pr.dma("sp", lambda en: en.dma_start(out=dst, in_=ZT[:, o, :, :]), "osp", reads=["ZT%d" % o])
            out_proj_ln(l, ti, c0, N)

        def odd_sample_tile(l, ti, c0):
            o = l // 2
            N = NB
            PZP = PS[:, 0:24 * N].rearrange("p (a b) -> p a b", a=24)
            for oc in range(24):
                inproj_fm(oc, c0, N, ti, 0, col=oc * N)
            act(PZS, PZP, AF.Copy, [("ps", 0)], ["PZS"])
            PZ = PZS
            SS = T[0][:, 0:8 * N * 2].rearrange("p (a b c) -> p a b c", a=8, b=N)
            OS = T[1][:, 0:8 * N * 2].rearrange("p (a b c) -> p a b c", a=8, b=N)
            A = T[2][:, 0:8 * N].rearrange("p (a b) -> p a b", a=8)
            Zs = T[3][:, 0:8 * N].rearrange("p (a b) -> p a b", a=8)
            Y = T[4][:, 0:8 * N].rearrange("p (a b) -> p a b", a=8)
            src = s_sconv[o].rearrange("(kc p) b k -> p kc (b k)", p=128)
            pr.dma("sp", lambda en: en.dma_start(out=T[0][:, 0:8 * N * 2].rearrange("p (a b) -> p a b", a=8), in_=src),
                   "ssin", writes=[("T", 0)], bar=True)
            act(A, PZ[:, 8:16, :], AF.Copy, ["PZS"], [("T", 2)])
            tt(Zs, PZ[:, 16:24, :], A, ALU.mult, ["PZS", ("T", 2)], [("T", 3)])
            for kc in range(8):
                w = lambda k: PV[:, PV_SCW + (o * 8 + kc) * 3 + k:PV_SCW + (o * 8 + kc) * 3 + k + 1]
                ts1(Y[:, kc, :], Zs[:, kc, :], w(2), ALU.mult, [("T", 3), "PV"], [("T", 4)])
                stt(Y[:, kc, :], SS[:, kc, :, 1], w(1), Y[:, kc, :], ALU.mult, ALU.add, [("T", 0), "PV", ("T", 4)],
                    [("T", 4)])
                stt(Y[:, kc, :], SS[:, kc, :, 0], w(0), Y[:, kc, :], ALU.mult, ALU.add, [("T", 0), "PV", ("T", 4)],
                    [("T", 4)])
            tt(M[:, :, 0:N], PZ[:, 0:8, :], Y, ALU.mult, ["PZS", ("T", 4)], ["M"])
            act(OS[:, :, :, 0], SS[:, :, :, 1], AF.Copy, [("T", 0)], [("T", 1)])
            act(OS[:, :, :, 1], Zs, AF.Copy, [("T", 3)], [("T", 1)])
            dst = o_sconv_s[o].rearrange("(kc p) b k -> p kc (b k)", p=128)
            pr.dma("sp", lambda en: en.dma_start(out=dst, in_=T[1][:, 0:8 * N * 2].rearrange("p (a b) -> p a b", a=8)),
                   "ssout", reads=[("T", 1)], bar=True)
            out_proj_ln(l, ti, c0, N)

        def ffn_phase(l, tiles, next_win, is_last):
            pr.barrier()
            loads = []
            for g in range(2):
                for bi in range(3):
                    loads.append(("13", g, bi))
                for ob in range(4):
                    loads.append(("2", g, ob))
            issued = {"13": 0, "2": 0}
            seq = {"13": [x for x in loads if x[0] == "13"], "2": [x for x in loads if x[0] == "2"]}
            win_left = list(range(3)) if next_win is not None else []

            def issue(kind, idx):
                _, g, bi = seq[kind][idx]
                sl = idx % 2
                if kind == "13":
                    j0 = g * NJG + bi * 4
                    nj = min(4, g * NJG + NJG - j0)
                    for nm, wsrc, dstt in (("w1", ffn_w1, W1S), ("w3", ffn_w3, W3S)):
                        src = wsrc[l].rearrange("(kc p) n -> p kc n", p=128)[:, :, j0 * 128:(j0 + nj) * 128]
                        dst = dstt[sl][:, :, 0:nj * 128]
                        pr.dma("pool", lambda en, s=src, d=dst: en.dma_start(out=d, in_=s), "%s_%d" % (nm, sl),
                               writes=[(nm, sl)])
                else:
                    src = ffn_w2[l].rearrange("(j p) n -> p j n", p=128)[:, g * NJG:(g + 1) * NJG, bi * 256:(bi + 1) * 256]
                    dst = W2S[sl][:, :, :]
                    pr.dma("pool", lambda en, s=src, d=dst: en.dma_start(out=d, in_=s), "w2_%d" % sl,
                           writes=[("W2O", sl)])
                if win_left:
                    issue_win_piece(next_win, win_left.pop(0))

            def ensure(kind, idx):
                while issued[kind] <= min(idx + 1, len(seq[kind]) - 1):
                    issue(kind, issued[kind])
                    issued[kind] += 1

            ensure("13", 0)
            while pending_ln:
                pending_ln.pop(0)()
            i13 = 0
            i2 = 0
            pairs = [(0, 1), (2, 3), (4, 5), (6, 7)]
            pi = 0
            for g in range(2):
                for bi in range(3):
                    ensure("13", i13)
                    if bi == 1:
                        ensure("2", i2 - 1 if i2 > 0 else 0)
                    sl = i13 % 2
                    j0 = g * NJG + bi * 4
                    nj = min(4, g * NJG + NJG - j0)
                    for jj in range(nj):
                        jl = j0 + jj - g * NJG
                        for (ti, c0, N, dc, kind, gid) in tiles:
                            ba, bb = pairs[pi % 4]
                            pi += 1
                            mm_group(bank(ba, N), [(W1S[sl][:, kc, jj * 128:(jj + 1) * 128], XB[:, kc, c0:c0 + N])
                                                  for kc in range(8)], [("w1", sl), ("XB", ti)], ("ps", ba))
                            mm_group(bank(bb, N), [(W3S[sl][:, kc, jj * 128:(jj + 1) * 128], XB[:, kc, c0:c0 + N])
                                                  for kc in range(8)], [("w3", sl), ("XB", ti)], ("ps", bb))
                            ft = FT[pi % 2][:, 0:N]
                            act(ft, bank(ba, N), AF.Silu, [("ps", ba)], [("FT", pi % 2)])
                            tt(H[:, jl, c0:c0 + N], ft, bank(bb, N), ALU.mult, [("FT", pi % 2), ("ps", bb)],
                               [("H", ti)])
                    i13 += 1
                for ob in range(4):
                    ensure("2", i2)
                    sl = i2 % 2
                    for ol in range(2):
                        oc = ob * 2 + ol
                        for (ti, c0, N, dc, kind, gid) in tiles:
                            b = nb(6)
                            mm_group(bank(b, N), [(W2S[sl][:, jl, ol * 128:(ol + 1) * 128], H[:, jl, c0:c0 + N])
                                                 for jl in range(NJG)], [("W2O", sl), ("H", ti)], ("ps", b))
                            x = XF[:, oc, c0:c0 + N]
                            if g == 0:
                                stt(x, x, ALPHA, bank(b, N), ALU.mult, ALU.add, [("XF", ti, oc), ("ps", b)],
                                    [("XF", ti, oc)])
                            else:
                                tt(x, x, bank(b, N), ALU.add, [("XF", ti, oc), ("ps", b)], [("XF", ti, oc)])
                    i2 += 1
            while win_left:
                issue_win_piece(next_win, win_left.pop(0))
            pr.barrier()
            for (ti, c0, N, dc, kind, gid) in tiles:
                deepnorm_ln(ti, c0, N, PV_LNFW + l * 8, PV_LNFB + l * 8, dc if is_last else None)

        segs = [
            [(0, 0, 512, 0, "p", 0), (1, 512, 512, 512, "p", 1), (2, 1024, NB, 2048, "s", 4)],
            [(0, 0, 512, 1024, "p", 2), (1, 512, 512, 1536, "p", 3)],
        ]
        LAYERS = list(range(NL)) if isinstance(NL, int) else list(NL)
        import os as _os
        for _i in range(int(_os.environ.get("DUMMY_DVE", "0"))):
            pr.op("dve", lambda e: e.memset(BIASC[:, 3:4], 0.0), writes=["dummy"])
        for _i in range(int(_os.environ.get("DUMMY_ACT", "0"))):
            pr.op("act", lambda e: e.activation(out=BIASC[:, 2:3], in_=BIASC[:, 0:1], func=AF.Copy), writes=["dummy2"])
        for _i in range(int(_os.environ.get("DUMMY_SP", "0"))):
            pr.dma("sp", lambda e: e.dma_start(out=PV[:, 458:460], in_=pv_d[:, 458:460]), "dummysp", writes=["dummy3"])
        for _i in range(int(_os.environ.get("DUMMY_POOL", "0"))):
            pr.dma("pool", lambda e: e.dma_start(out=PV[:, 456:458], in_=pv_d[:, 456:458]), "dummypool", writes=["dummy4"])
        if _os.environ.get("ONESEG"):
            segs = segs[:1]
        if _os.environ.get("NOSAMPLE"):
            segs[0] = segs[0][:2]
        for i in range(3):
            issue_win_piece(LAYERS[0], i)
        for si, tiles in enumerate(segs):
            for (ti, c0, N, dc, kind, gid) in tiles:
                src = xT.rearrange("(kc p) t -> p kc t", p=128)[:, :, dc:dc + N]
                pr.dma("sp", lambda en, s=src, c0=c0, N=N: en.dma_start(out=XF[:, :, c0:c0 + N], in_=s), "x%d" % ti,
                       writes=[("XF", ti, oc) for oc in range(8)])
                act(XB[:, :, c0:c0 + N], XF[:, :, c0:c0 + N], AF.Copy, [("XF", ti, oc) for oc in range(8)], [("XB", ti)])
            for li, l in enumerate(LAYERS):
                issue_wout(l)
                for tix, (ti, c0, N, dc, kind, gid) in enumerate(tiles):
                    lastp = (gid == 3)
                    defer_flag[0] = (tix == len(tiles) - 1)
                    if kind == "s":
                        pr.barrier()
                    if l % 2 == 0:
                        if kind == "p":
                            even_prompt_tile(l, ti, c0, lastp)
                        else:
                            even_sample_tile(l, ti, c0)
                    else:
                        if kind == "p":
                            odd_prompt_tile(l, ti, c0, lastp)
                        else:
                            odd_sample_tile(l, ti, c0)
                    defer_flag[0] = False
                    if kind == "s" and tix != len(tiles) - 1:
                        pr.barrier()
                if li + 1 < len(LAYERS):
                    nxt = LAYERS[li + 1]
                elif si + 1 < len(segs):
                    nxt = LAYERS[0]
                else:
                    nxt = None
                ffn_phase(l, tiles, nxt, li == len(LAYERS) - 1)
        pr.finish()
        with nc.Block() as block:
            pr.emit(block)
    return nc


def _fm(a, nch):
    lead = a.shape[:-1]
    a = a.reshape(lead + (nch, 128))
    nd = a.ndim
    return np.ascontiguousarray(np.transpose(a, (nd - 1,) + tuple(range(nd - 2)) + (nd - 2,)))


def _pack_pv(inp):
    pv = np.zeros((128, NPV), np.float32)
    pv[:, PV_LNMW:PV_LNMW + 32] = _fm(inp["ln_mix_w"], 8).reshape(128, 32)
    pv[:, PV_LNMB:PV_LNMB + 32] = _fm(inp["ln_mix_b"], 8).reshape(128, 32)
    pv[:, PV_LNFW:PV_LNFW + 32] = _fm(inp["ln_ffn_w"], 8).reshape(128, 32)
    pv[:, PV_LNFB:PV_LNFB + 32] = _fm(inp["ln_ffn_b"], 8).reshape(128, 32)
    dw = _fm(inp["conf_dw_w"], 4)
    pv[:, PV_DWW:PV_DWW + 248] = np.transpose(dw, (0, 1, 3, 2)).reshape(128, 248)
    pv[:, PV_DWB:PV_DWB + 8] = _fm(inp["conf_dw_b"], 4).reshape(128, 8)
    pv[:, PV_CLW:PV_CLW + 8] = _fm(inp["conf_ln_w"], 4).reshape(128, 8)
    pv[:, PV_CLB:PV_CLB + 8] = _fm(inp["conf_ln_b"], 4).reshape(128, 8)
    scw = _fm(inp["sc_conv_w"], 8)
    pv[:, PV_SCW:PV_SCW + 48] = np.transpose(scw, (0, 1, 3, 2)).reshape(128, 48)
    pv[:, PV_LBL:PV_LBL + 8] = _fm(inp["hgrn_lb_logits"], 4).reshape(128, 8)
    pv[:, PV_GNW:PV_GNW + 2] = np.ascontiguousarray(inp["hgrn_gnorm_w"].T)
    return pv


def _consts():
    cf = np.zeros((128, NCST), np.float32)
    cb = np.zeros((128, NCST), np.float32)
    cf[:, C_ID:C_ID + 128] = np.eye(128, dtype=np.float32)
    cb[:, C_ID:C_ID + 128] = np.eye(128, dtype=np.float32)
    p = np.arange(128)[:, None] % 64
    t = np.arange(512)[None, :] % 64
    cb[:, C_TRI:C_TRI + 512] = (t >= p).astype(np.float32)
    cf[:, C_SM:C_SM + 512] = ((np.arange(512) % 64) != 0).astype(np.float32)[None, :]
    cf[:, C_ONE:C_ONE + 128] = 1.0
    cb[:, C_ONE:C_ONE + 128] = 1.0
    return cf, cb


_NC_CACHE = {}


def kernel(NL=DEPTH, **inp):
    inp = {k: np.asarray(v) for k, v in inp.items()}
    if NL not in _NC_CACHE:
        _NC_CACHE[NL] = build(NL)
    nc = _NC_CACHE[NL]
    pv = _pack_pv(inp)
    cstf, cstb = _consts()
    in_maps = []
    for c in range(NCORES):
        bs = slice(c * NB, (c + 1) * NB)
        xT = np.empty((D, NTOK), np.float32)
        xT[:, :2048] = inp["x_prompt"][c].T
        xT[:, 2048:] = inp["x_sample"][bs, 0, :].T
        in_maps.append({
            "xT": xT,
            "s_hgrn": np.ascontiguousarray(np.transpose(inp["state_hgrn"][:, bs], (0, 3, 1, 2, 4))),
            "s_conf": np.ascontiguousarray(np.transpose(inp["state_conf"][:, bs], (0, 3, 1, 2))),
            "s_sconv": np.ascontiguousarray(np.transpose(inp["state_sconv"][:, bs], (0, 3, 1, 2))),
            "w_in_even": inp["w_in_even"], "w_out_even": inp["w_out_even"],
            "sc_w_in": inp["sc_w_in"], "sc_w_out": inp["sc_w_out"],
            "ffn_w1": inp["ffn_w1"], "ffn_w3": inp["ffn_w3"], "ffn_w2": inp["ffn_w2"],
            "pv": pv, "cstf": cstf, "cstb": cstb,
        })
    res = run_bass_kernel_spmd(nc, in_maps, core_ids=list(range(NCORES)))
    R = res.results
    y_prompt = np.stack([R[c]["yT"][:, :2048].T for c in range(NCORES)])
    y_sample = np.concatenate([R[c]["yT"][:, 2048:].T for c in range(NCORES)])[:, None, :]
    h_p = np.stack([R[c]["o_hgrn_p"] for c in range(NCORES)], axis=1)
    c_p = np.stack([np.transpose(R[c]["o_conf_p"], (0, 2, 1)) for c in range(NCORES)], axis=1)
    s_p = np.stack([np.transpose(R[c]["o_sconv_p"], (0, 2, 1)) for c in range(NCORES)], axis=1)
    h_s = np.concatenate([np.transpose(R[c]["o_hgrn_s"], (0, 2, 3, 1, 4)) for c in range(NCORES)], axis=1)
    c_s = np.concatenate([np.transpose(R[c]["o_conf_s"], (0, 2, 3, 1)) for c in range(NCORES)], axis=1)
    s_s = np.concatenate([np.transpose(R[c]["o_sconv_s"], (0, 2, 3, 1)) for c in range(NCORES)], axis=1)
    f = lambda a: np.ascontiguousarray(a, dtype=np.float32)
    return (f(y_prompt), f(y_sample), f(h_p), f(c_p), f(s_p), f(h_s), f(c_s), f(s_s))
```

```python
import math
from contextlib import ExitStack
import numpy as np
import concourse.bass as bass
import concourse.mybir as mybir
from concourse.bass_utils import run_bass_kernel_spmd

F32 = mybir.dt.float32
BF16 = mybir.dt.bfloat16
AF = mybir.ActivationFunctionType
ALU = mybir.AluOpType
AX = mybir.AxisListType

NCORES = 8
D = 1024
KC = 8
DFF = 2816
NJ = 22
NJG = 11
EIN = 3072
DEPTH = 4
ALPHA = (2 * DEPTH) ** 0.25
LN_EPS = 1e-5
RMS_EPS = 1e-6
QSCALE = 128 ** -0.5
SW = 1040
NTOK = 2064
NB = 16

PV_LNMW, PV_LNMB, PV_LNFW, PV_LNFB = 0, 32, 64, 96
PV_DWW = 128
PV_DWB = 376
PV_CLW = 384
PV_CLB = 392
PV_SCW = 400
PV_LBL = 448
PV_GNW = 456
NPV = 460
C_ID = 0
C_TRI = 128
C_SM = 128
C_ONE = 640
NCST = 768


class Prog:
    ENG = ("pe", "act", "dve", "pool", "sp")

    def __init__(self, nc, es):
        self.nc = nc
        self.es = es
        self.q = {e: [] for e in self.ENG}
        self.semh = {e: es.enter_context(nc.semaphore("s_" + e)) for e in self.ENG}
        self.cnt = {e: 0 for e in self.ENG}
        self.seen = {e: {} for e in self.ENG}
        self.lastw = {}
        self.readers = {}
        self.dtot = {}
        self.dbar = {}

    def _deps(self, eng, reads, writes):
        deps = []
        for b in reads:
            if b in self.lastw:
                deps.append(self.lastw[b])
        for b in writes:
            if b in self.lastw:
                deps.append(self.lastw[b])
            deps.extend(self.readers.get(b, ()))
        out = []
        for (sk, val) in deps:
            if sk == "pe" and eng == "pe":
                continue
            if self.seen[eng].get(sk, 0) >= val:
                continue
            self.seen[eng][sk] = val
            out.append((sk, val))
        return out

    def _book(self, me, reads, writes):
        for b in writes:
            self.lastw[b] = me
            self.readers[b] = []
        for b in reads:
            self.readers.setdefault(b, []).append(me)

    def op(self, eng, fn, reads=(), writes=(), signal=True):
        waits = self._deps(eng, reads, writes)
        if signal:
            self.cnt[eng] += 1
            me = (eng, self.cnt[eng])
        else:
            me = (eng, self.cnt[eng] + 1)
        self.q[eng].append((waits, fn, "sig" if signal else None))
        self._book(me, reads, writes)

    def dma(self, eng, fn, sem, reads=(), writes=(), bar=False):
        waits = self._deps(eng, reads, writes)
        key = "d:" + sem
        if key not in self.semh:
            self.semh[key] = self.es.enter_context(self.nc.semaphore("d_" + sem))
            self.dtot[key] = 0
        self.dtot[key] += 16
        self.dbar[key] = bar
        me = (key, self.dtot[key])
        if eng == "sp":
            hist = self.__dict__.setdefault("sp_hist", [])
            if len(hist) >= 2:
                pk, pv_ = hist[-2]
                if pk != key and self.seen[eng].get(pk, 0) < pv_:
                    self.seen[eng][pk] = pv_
                    waits.append((pk, pv_))
            hist.append(me)
        self.q[eng].append((waits, fn, key))
        self._book(me, reads, writes)

    def barrier(self):
        snap = dict(self.cnt)
        dsn = {k: v for k, v in self.dtot.items() if self.dbar.get(k)}
        for e in self.ENG:
            waits = []
            for e2 in self.ENG:
                if e2 == e or snap[e2] == 0:
                    continue
                if self.seen[e].get(e2, 0) < snap[e2]:
                    self.seen[e][e2] = snap[e2]
                    waits.append((e2, snap[e2]))
            for k, v in dsn.items():
                if self.seen[e].get(k, 0) < v:
                    self.seen[e][k] = v
                    waits.append((k, v))
            if waits:
                self.q[e].append((waits, None, None))
        for b in list(self.lastw.keys()):
            if not self.lastw[b][0].startswith("d:"):
                del self.lastw[b]
        for b in list(self.readers.keys()):
            self.readers[b] = [r for r in self.readers[b] if r[0].startswith("d:") and not self.dbar.get(r[0])]

    def finish(self):
        waits = [(k, v) for k, v in self.dtot.items()]
        waits += [(e, self.cnt[e]) for e in self.ENG if e != "sp" and self.cnt[e] > 0]
        self.q["sp"].append((waits, None, None))

    def emit(self, block):
        for e, attr in (("pe", "tensor"), ("act", "scalar"), ("dve", "vector"), ("pool", "gpsimd"), ("sp", "sync")):
            def body(engobj, e=e):
                for waits, fn, sig in self.q[e]:
                    for (sk, val) in waits:
                        engobj.wait_ge(self.semh[sk], val)
                    if fn is None:
                        continue
                    ins = fn(engobj)
                    if sig == "sig":
                        ins.then_inc(self.semh[e], 1)
                    elif sig is not None:
                        ins.then_inc(self.semh[sig], 16)
            getattr(block, attr)(body)


def build(NL=DEPTH):
    nc = bass.Bass("TRN2", target_bir_lowering=False)

    def din(name, shape):
        return nc.dram_tensor(name, shape, F32, kind="ExternalInput").ap()

    def dout(name, shape):
        return nc.dram_tensor(name, shape, F32, kind="ExternalOutput").ap()

    xT = din("xT", [D, NTOK])
    s_hgrn = din("s_hgrn", [2, 128, NB, 4, 128])
    s_conf = din("s_conf", [2, 512, NB, 30])
    s_sconv = din("s_sconv", [2, D, NB, 2])
    w_in_even = din("w_in_even", [2, D, EIN])
    w_out_even = din("w_out_even", [2, D, D])
    sc_w_in = din("sc_w_in", [2, D, EIN])
    sc_w_out = din("sc_w_out", [2, D, D])
    ffn_w1 = din("ffn_w1", [DEPTH, D, DFF])
    ffn_w3 = din("ffn_w3", [DEPTH, D, DFF])
    ffn_w2 = din("ffn_w2", [DEPTH, DFF, D])
    pv_d = din("pv", [128, NPV])
    cstf_d = din("cstf", [128, NCST])
    cstb_d = din("cstb", [128, NCST])

    yT = dout("yT", [D, NTOK])
    o_hgrn_p = dout("o_hgrn_p", [2, 4, 128, 128])
    o_conf_p = dout("o_conf_p", [2, 512, 30])
    o_sconv_p = dout("o_sconv_p", [2, D, 2])
    o_hgrn_s = dout("o_hgrn_s", [2, 128, NB, 4, 128])
    o_conf_s = dout("o_conf_s", [2, 512, NB, 30])
    o_sconv_s = dout("o_sconv_s", [2, D, NB, 2])

    es = ExitStack()
    with es:
        def sb(name, shape, dt=F32):
            return es.enter_context(nc.sbuf_tensor(name, shape, dt))

        XF = sb("XF", [128, KC, SW])
        XB = sb("XB", [128, KC, SW], BF16)
        PV = sb("PV", [128, NPV])
        CF = sb("CF", [128, NCST])
        CB = sb("CB", [128, NCST], BF16)
        LBV = sb("LBV", [128, 3, 2, 4])
        ZST = sb("ZST", [128, 2, 4, 128])
        SBF = sb("SBF", [128, 4, 128], BF16)
        EBA = sb("EBA", [128, 2, 4, 9])
        UT = sb("UT", [128, 2, 4, 30])
        ZT = sb("ZT", [128, 2, 8, 2])
        SFIN = sb("SFIN", [128, 4, 128])
        WIN = sb("WIN", [128, KC, EIN], BF16)
        W2O = sb("W2O", [128, 8192], BF16)
        SCRB = 74240
        SCR = sb("SCR", [128, SCRB // 4])
        SCRH = SCR.bitcast(BF16)
        PS = es.enter_context(nc.psum_tensor("PS", [128, 4096], F32))
        PSH = PS.bitcast(BF16)

        pr = Prog(nc, es)

        IDB = CB[:, C_ID:C_ID + 128]
        ONEB = CB[:, C_ONE:C_ONE + 128]
        TRI = CB[:, C_TRI:C_TRI + 512]
        SMASK = CF[:, C_SM:C_SM + 512]
        IDF = CF[:, C_ID:C_ID + 128]
        ONEF = CF[:, C_ONE:C_ONE + 128]

        off = [0]

        def carve(nbytes):
            o = off[0]
            off[0] += (nbytes + 63) // 64 * 64
            return o

        def vf(o, n):
            return SCR[:, o // 4:o // 4 + n]

        def vh(o, n):
            return SCRH[:, o // 2:o // 2 + n]

        T = [vf(carve(2048), 512) for _ in range(6)]
        o_alias = off[0]
        ACC = vf(carve(8192), 2048).rearrange("p (a b) -> p a b", a=4)
        QT = vh(carve(4096), 2048).rearrange("p (a b) -> p a b", a=4)
        KT = vh(carve(4096), 2048).rearrange("p (a b) -> p a b", a=4)
        VT = vh(carve(4096), 2048).rearrange("p (a b) -> p a b", a=4)
        SG = vh(carve(4096), 2048).rearrange("p (a b) -> p a b", a=4)
        KTT = vh(carve(4096), 2048)
        SC = vh(carve(2048), 1024)
        o_alias_end = off[0]
        M = vh(carve(8192), 4096).rearrange("p (a b) -> p a b", a=8)
        o_u = carve(8704)
        UREG = vf(o_u, 4 * 544)
        U = vh(o_u, 4 * 542).rearrange("p (a b) -> p a b", a=4)
        DG = [vh(o_u + 4352 + i * 256, 128) for i in range(16)]
        o_ln = off[0]
        RBT = [vh(carve(1024), 512) for _ in range(2)]
        R2T = [vh(carve(1024), 512) for _ in range(2)]
        MU = vf(carve(2048), 512)
        RS = vf(carve(2048), 512)
        MSQ = vf(carve(2048), 512)
        VSB = vf(carve(2048), 512)
        VMB = vf(carve(2048), 512)
        assert off[0] <= SCRB, off[0]
        off[0] = o_alias
        SRF = [vf(carve(8192), 2048) for _ in range(2)]
        SR = [x.rearrange("p (q a b) -> p q a b", q=4, a=4) for x in SRF]
        UHF = vf(carve(7680), 4 * 16 * 30)
        UH = UHF.rearrange("p (a b c) -> p a b c", a=4, b=16)
        TMPC = vf(carve(1920), 16 * 30).rearrange("p (b c) -> p b c", b=16)
        KV = vf(carve(2048), 512).rearrange("p (a b) -> p a b", a=4)
        PZS = vf(carve(1536), 384).rearrange("p (a b) -> p a b", a=24)
        POS = vf(carve(256), 64).rearrange("p (a b) -> p a b", a=4)
        assert off[0] <= o_alias_end, (off[0], o_alias_end)
        off[0] = o_alias
        ZB = vf(carve(8 * 514 * 4), 8 * 514).rearrange("p (a b) -> p a b", a=8)
        assert off[0] <= o_alias_end
        off[0] = 0
        H = vh(carve(NJG * SW * 2), NJG * SW).rearrange("p (a b) -> p a b", a=NJG)
        W1S = [vh(carve(8192), 4096).rearrange("p (a b) -> p a b", a=8) for _ in range(2)]
        W3S = [vh(carve(8192), 4096).rearrange("p (a b) -> p a b", a=8) for _ in range(2)]
        FT = [vf(carve(2048), 512) for _ in range(2)]
        o_ffn_end = off[0]
        assert o_ffn_end <= o_ln
        WOUT = W2O[:, 0:8192].rearrange("p (a b) -> p a b", a=8)
        W2S = [W2O[:, i * 2816:(i + 1) * 2816].rearrange("p (a b) -> p a b", a=NJG) for i in range(2)]

        def bank(b, n=512, p0=0, p1=128):
            return PS[p0:p1, b * 512:b * 512 + n]

        def act(out, in_, func, reads, writes, scale=None, bias=None):
            kw = {}
            if scale is not None:
                kw["scale"] = scale
            if bias is not None:
                kw["bias"] = bias
            pr.op("act", lambda e: e.activation(out=out, in_=in_, func=func, **kw), reads, writes)

        def tt(out, a, b, op, reads, writes):
            pr.op("dve", lambda e: e.tensor_tensor(out=out, in0=a, in1=b, op=op), reads, writes)

        def ts(out, a, s1, s2, op0, op1, reads, writes):
            pr.op("dve", lambda e: e.tensor_scalar(out=out, in0=a, scalar1=s1, scalar2=s2, op0=op0, op1=op1),
                  reads, writes)

        def ts1(out, a, s1, op0, reads, writes):
            pr.op("dve", lambda e: e.tensor_scalar(out=out, in0=a, scalar1=s1, scalar2=None, op0=op0), reads, writes)

        def stt(out, a, s, b, op0, op1, reads, writes):
            pr.op("dve", lambda e: e.scalar_tensor_tensor(out=out, in0=a, scalar=s, in1=b, op0=op0, op1=op1),
                  reads, writes)

        def mm(out, lhsT, rhs, start, stop, reads, writes, signal):
            pr.op("pe", lambda e: e.matmul(out, lhsT=lhsT, rhs=rhs, start=start, stop=stop), reads, writes, signal)

        def mm_group(out, pairs, reads, wkey):
            n = len(pairs)
            for i, (l, r) in enumerate(pairs):
                mm(out, l, r, i == 0, i == n - 1, reads, [wkey], i == n - 1)

        cst_biases = {}

        def fbias(val):
            if val not in cst_biases:
                cst_biases[val] = len(cst_biases)
            return BIASC[:, cst_biases[val]:cst_biases[val] + 1]

        BIASC = sb("BIASC", [128, 4])

        pr.dma("sp", lambda e: e.dma_start(out=PV[:, :], in_=pv_d[:, :]), "pv", writes=["PV"])
        pr.dma("sp", lambda e: e.dma_start(out=CF[:, :], in_=cstf_d[:, :]), "cf", writes=["CF"])
        pr.dma("pool", lambda e: e.dma_start(out=CB[:, :], in_=cstb_d[:, :]), "cb", writes=["CB"])
        pr.op("dve", lambda e: e.memset(ZST[:, :, :, :], 0.0), writes=[("Z%d" % e_, h_) for e_ in range(2) for h_ in range(4)])
        pr.op("dve", lambda e: e.memset(EBA[:, :, :, :], 1.0), writes=["EBA0", "EBA1"])
        pr.op("dve", lambda e: e.memset(UT[:, :, :, :], 0.0), writes=["UT0", "UT1"])
        pr.op("dve", lambda e: e.memset(ZT[:, :, :, :], 0.0), writes=["ZT0", "ZT1"])
        pr.op("dve", lambda e: e.memset(SBF[:, :, :], 0.0), writes=[("SBF", h_) for h_ in range(4)])
        pr.op("dve", lambda e: e.memset(BIASC[:, 0:1], LN_EPS), writes=["BIASC"])
        pr.op("dve", lambda e: e.memset(BIASC[:, 1:2], RMS_EPS), writes=["BIASC"])
        cst_biases[LN_EPS] = 0
        cst_biases[RMS_EPS] = 1
        pr.op("dve", lambda e: e.memset(LBV[:, 0, 0, :], 0.0), writes=["LBV"])
        tt(LBV[:, 0, 1, :], PV[:, PV_LBL + 4:PV_LBL + 8], PV[:, PV_LBL:PV_LBL + 4], ALU.subtract, ["PV"], ["LBV"])
        act(LBV[:, 0, 1, :], LBV[:, 0, 1, :], AF.Sigmoid, ["LBV"], ["LBV"])
        ts(LBV[:, 1, :, :], LBV[:, 0, :, :], -1.0, 1.0, ALU.mult, ALU.add, ["LBV"], ["LBV"])
        ts1(LBV[:, 2, :, :], LBV[:, 1, :, :], -1.0, ALU.mult, ["LBV"], ["LBV"])

        win_src = []
        wout_src = []
        for l in range(DEPTH):
            if l % 2 == 0:
                win_src.append(w_in_even[l // 2])
                wout_src.append(w_out_even[l // 2])
            else:
                win_src.append(sc_w_in[l // 2])
                wout_src.append(sc_w_out[l // 2])

        def issue_win_piece(l, i):
            src = win_src[l].rearrange("(kc p) n -> p kc n", p=128)[:, :, i * 1024:(i + 1) * 1024]
            dst = WIN[:, :, i * 1024:(i + 1) * 1024]
            pr.dma("pool", lambda e: e.dma_start(out=dst, in_=src), "win%d" % i, writes=[("WIN", i)])

        def issue_wout(l):
            src = wout_src[l].rearrange("(kc p) n -> p kc n", p=128)
            pr.dma("pool", lambda e, s=src: e.dma_start(out=WOUT[:, :, :], in_=s), "wout",
                   writes=[("W2O", 0), ("W2O", 1)])

        def ln_stats(srcs, N, inv_n, eps, bm, bv):
            n = len(srcs)
            for i, (ap, key) in enumerate(srcs):
                rb = RBT[i % 2][:, 0:N]
                r2 = R2T[i % 2][:, 0:N]
                act(rb, ap, AF.Copy, [key], [("RBT", i % 2)])
                act(r2, ap, AF.Square, [key], [("R2T", i % 2)])
                mm(bank(bm, N), ONEB, rb, i == 0, i == n - 1, [("RBT", i % 2), "CB"], [("ps", bm)], True)
                mm(bank(bv, N), ONEB, r2, i == 0, i == n - 1, [("R2T", i % 2), "CB"], [("ps", bv)], True)
            ts1(MU[:, 0:N], bank(bm, N), inv_n, ALU.mult, [("ps", bm)], ["MU"])
            tt(MSQ[:, 0:N], MU[:, 0:N], MU[:, 0:N], ALU.mult, ["MU"], ["MSQ"])
            stt(MSQ[:, 0:N], bank(bv, N), inv_n, MSQ[:, 0:N], ALU.mult, ALU.subtract, [("ps", bv), "MSQ"], ["MSQ"])
            act(MSQ[:, 0:N], MSQ[:, 0:N], AF.Ln, ["MSQ", "BIASC"], ["MSQ"], bias=fbias(eps))
            act(RS[:, 0:N], MSQ[:, 0:N], AF.Exp, ["MSQ"], ["RS"], scale=-0.5)

        def deepnorm_ln(ti, c0, N, wcol, bcol, out_cols):
            srcs = [(XF[:, oc, c0:c0 + N], ("XF", ti, oc)) for oc in range(8)]
            ln_stats(srcs, N, 1.0 / D, LN_EPS, 6, 7)
            for oc in range(8):
                x = XF[:, oc, c0:c0 + N]
                k = ("XF", ti, oc)
                tt(x, x, MU[:, 0:N], ALU.subtract, [k, "MU"], [k])
                tt(x, x, RS[:, 0:N], ALU.mult, [k, "RS"], [k])
                act(x, x, AF.Identity, [k, "PV"], [k], scale=PV[:, wcol + oc:wcol + oc + 1],
                    bias=PV[:, bcol + oc:bcol + oc + 1])
                act(XB[:, oc, c0:c0 + N], x, AF.Copy, [k], [("XB", ti)])
            if out_cols is not None:
                dst = yT.rearrange("(kc p) t -> p kc t", p=128)[:, :, out_cols:out_cols + N]
                pr.dma("sp", lambda e: e.dma_start(out=dst, in_=XF[:, :, c0:c0 + N]), "y%d" % ti,
                       reads=[("XF", ti, oc) for oc in range(8)])

        pending_ln = []

        def out_proj_ln(l, ti, c0, N):
            for oc in range(8):
                b = oc % 4
                mm_group(bank(b, N), [(WOUT[:, kc, oc * 128:(oc + 1) * 128], M[:, kc, 0:N]) for kc in range(8)],
                         [("W2O", 0), "M"], ("ps", b))
                x = XF[:, oc, c0:c0 + N]
                stt(x, x, ALPHA, bank(b, N), ALU.mult, ALU.add, [("XF", ti, oc), ("ps", b)], [("XF", ti, oc)])
            if defer_flag[0]:
                pending_ln.append(lambda: deepnorm_ln(ti, c0, N, PV_LNMW + l * 8, PV_LNMB + l * 8, None))
            else:
                deepnorm_ln(ti, c0, N, PV_LNMW + l * 8, PV_LNMB + l * 8, None)

        defer_flag = [False]
        rot = [0]

        def nb(nbanks=4, base=0):
            rot[0] = (rot[0] + 1) % nbanks
            return base + rot[0]

        def inproj_fm(oc, c0, N, ti, b, col=0):
            out = PS[:, b * 512 + col:b * 512 + col + N]
            mm_group(out, [(WIN[:, kc, oc * 128:(oc + 1) * 128], XB[:, kc, c0:c0 + N]) for kc in range(8)],
                     [("WIN", oc // 8), ("XB", ti)], ("ps", b))

        def conf_ln_silu(e, N, accv):
            srcs = [(accv[:, cc, 0:N], "ACC") for cc in range(4)]
            ln_stats(srcs, N, 1.0 / 512, LN_EPS, 6, 7)
            for cc in range(4):
                a = accv[:, cc, 0:N]
                tt(a, a, MU[:, 0:N], ALU.subtract, ["ACC", "MU"], ["ACC"])
                tt(a, a, RS[:, 0:N], ALU.mult, ["ACC", "RS"], ["ACC"])
                act(M[:, 4 + cc, 0:N], a, AF.Silu, ["ACC", "PV"], ["M"],
                    scale=PV[:, PV_CLW + e * 4 + cc:PV_CLW + e * 4 + cc + 1],
                    bias=PV[:, PV_CLB + e * 4 + cc:PV_CLB + e * 4 + cc + 1])

        def rms_gate(e, h, N, o_ps, o_key, sg_ap, sg_key, sbank):
            r2 = R2T[h % 2][:, 0:N]
            act(r2, o_ps, AF.Square, [o_key], [("R2T", h % 2)])
            mm(bank(sbank, N), ONEB, r2, True, True, [("R2T", h % 2), "CB"], [("ps", sbank)], True)
            act(RS[:, 0:N], bank(sbank, N), AF.Ln, [("ps", sbank), "BIASC"], ["RS"], scale=1.0 / 128, bias=fbias(RMS_EPS))
            act(RS[:, 0:N], RS[:, 0:N], AF.Exp, ["RS"], ["RS"], scale=-0.5)
            tt(MU[:, 0:N], o_ps, RS[:, 0:N], ALU.mult, [o_key, "RS"], ["MU"])
            stt(M[:, h, 0:N], MU[:, 0:N], PV[:, PV_GNW + e:PV_GNW + e + 1], sg_ap, ALU.mult, ALU.mult,
                ["MU", "PV", sg_key], ["M"])

        def even_prompt_tile(l, ti, c0, last_prompt):
            e = l // 2
            N = 512
            lb = lambda h: LBV[:, 0, e, h:h + 1]
            omlb = lambda h: LBV[:, 1, e, h:h + 1]
            nomlb = lambda h: LBV[:, 2, e, h:h + 1]
            act(U[:, :, 0:30], UT[:, e, :, :], AF.Copy, ["UT%d" % e], ["U"])
            for cc in range(4):
                ba, bg = 2 * (cc % 2), 2 * (cc % 2) + 1
                inproj_fm(16 + cc, c0, N, ti, ba)
                inproj_fm(20 + cc, c0, N, ti, bg)
                tmp = T[cc % 2][:, 0:N]
                act(tmp, bank(bg, N), AF.Sigmoid, [("ps", bg)], [("T", cc % 2)])
                tt(U[:, cc, 30:30 + N], bank(ba, N), tmp, ALU.mult, [("ps", ba), ("T", cc % 2)], ["U"])
                tt(UT[:, e, cc, :], bank(ba, N)[:, N - 30:N], tmp[:, N - 30:N], ALU.mult,
                   [("ps", ba), ("T", cc % 2)], ["UT%d" % e])
            wcol = lambda cc, k: PV[:, PV_DWW + (e * 4 + cc) * 31 + k:PV_DWW + (e * 4 + cc) * 31 + k + 1]

            dgi = [0]

            def conv_taps(k0, k1):
                for k in range(k0, k1):
                    for cc in range(4):
                        i = dgi[0] % 16
                        dgi[0] += 1
                        ts1(DG[i], IDB, wcol(cc, k), ALU.mult, ["CB", "PV"], [("DG", i)])
                        mm(bank(4 + cc, N), DG[i], U[:, cc, k:k + N], k == 0, k == 30, [("DG", i), "U"],
                           [("ps", 4 + cc)], True)
            for h in range(4):
                bq, bf_, bg = nb(), nb(), nb()
                inproj_fm(0 + h, c0, N, ti, bq)
                inproj_fm(4 + h, c0, N, ti, bf_)
                inproj_fm(12 + h, c0, N, ti, bg)
                conv_taps(8 * h, min(8 * h + 8, 31) if h < 3 else 31)
                t1, t2, t3, t4, t5, t6 = [T[i][:, 0:N] for i in range(6)]
                act(t1, bank(bq, N), AF.Sigmoid, [("ps", bq)], [("T", 0)])
                tt(t1, bank(bq, N), t1, ALU.mult, [("ps", bq), ("T", 0)], [("T", 0)])
                act(t2, bank(bf_, N), AF.Sigmoid, [("ps", bf_)], [("T", 1)])
                ts(t3, t2, nomlb(h), omlb(h), ALU.mult, ALU.add, [("T", 1), "LBV"], [("T", 2)])
                act(t2, t2, AF.Ln, [("T", 1), "LBV"], [("T", 1)], scale=omlb(h), bias=lb(h))
                pr.op("dve", lambda en, t4=t4, t2=t2: en.tensor_tensor_scan(
                    out=t4, data0=SMASK[:, 0:N], data1=t2, initial=0.0, op0=ALU.mult, op1=ALU.add),
                    [("T", 1), "CF"], [("T", 3)])
                act(t5, t4, AF.Exp, [("T", 3)], [("T", 4)])
                act(t6, t4, AF.Exp, [("T", 3)], [("T", 5)], scale=-1.0)
                stt(QT[:, h, :], t1, QSCALE, t5, ALU.mult, ALU.mult, [("T", 0), ("T", 4)], ["QT"])
                tt(KT[:, h, :], t3, t6, ALU.mult, [("T", 2), ("T", 5)], ["KT"])
                act(EBA[:, e, h, 1:9], T[4][:, 63:512:64], AF.Copy, [("T", 4)], ["EBA%d" % e])
                act(t1, bank(bg, N), AF.Sigmoid, [("ps", bg)], [("T", 0)])
                tt(SG[:, h, :], bank(bg, N), t1, ALU.mult, [("ps", bg), ("T", 0)], ["SG"])
            for cc in range(4):
                act(ACC[:, cc, 0:N], bank(4 + cc, N), AF.Identity, [("ps", 4 + cc), "PV"], ["ACC"],
                    bias=PV[:, PV_DWB + e * 4 + cc:PV_DWB + e * 4 + cc + 1])
            for sub in range(4):
                b = nb()
                mm_group(bank(b), [(XB[:, kc, c0 + sub * 128:c0 + (sub + 1) * 128], WIN[:, kc, 1024:1536])
                                   for kc in range(8)], [("WIN", 1), ("XB", ti)], ("ps", b))
                act(VT[:, sub, :], bank(b), AF.Copy, [("ps", b)], ["VT"])
            for h in range(4):
                bs = 4 + h // 2
                bt = 6 + h // 2
                for c in range(8):
                    hp = (c % 2) * 64
                    col = bs * 512 + (h % 2) * 256 + (c // 2) * 64
                    mm(PS[hp:hp + 64, col:col + 64], KT[:, h, c * 64:(c + 1) * 64], QT[:, h, c * 64:(c + 1) * 64],
                       True, True, ["KT", "QT"], [("ps", bs)], c == 7)
                for c in range(8):
                    hp = (c % 2) * 64
                    col = bt * 1024 + (h % 2) * 512 + (c // 2) * 128
                    pr.op("pe", lambda en, hp=hp, col=col, h=h, c=c: en.transpose(
                        out=PSH[hp:hp + 64, col:col + 128], in_=KT[:, h, c * 64:(c + 1) * 64], identity=IDB),
                        ["KT", "CB"], [("ps", bt)], c == 7)
            for i in range(2):
                tt(SC[:, i * 512:(i + 1) * 512], bank(4 + i), TRI, ALU.mult, [("ps", 4 + i), "CB"], ["SC"])
                act(KTT[:, i * 1024:(i + 1) * 1024], PSH[:, (6 + i) * 1024:(7 + i) * 1024], AF.Copy,
                    [("ps", 6 + i)], ["KTT"])
            conf_ln_silu(e, N, ACC)
            if last_prompt:
                dstc = o_conf_p[e].rearrange("(cc p) k -> p cc k", p=128)
                pr.dma("sp", lambda en: en.dma_start(out=dstc, in_=UT[:, e, :, :]), "ocp", reads=["UT%d" % e])
            zk = "Z%d" % e
            for h in range(4):
                act(SBF[:, h, :], ZST[:, e, h, :], AF.Identity, [(zk, h), "EBA%d" % e], [("SBF", h)],
                    scale=EBA[:, e, h, 0:1])
            for c in range(8):
                hp = (c % 2) * 64
                sub = c // 2
                for h in range(4):
                    o_out = PS[:, h * 512 + c * 64:h * 512 + (c + 1) * 64]
                    mm(o_out, VT[hp:hp + 64, sub, h * 128:(h + 1) * 128],
                       SC[hp:hp + 64, h * 256 + sub * 64:h * 256 + (sub + 1) * 64], True, False,
                       ["VT", "SC"], [("ps", h)], False)
                    bu = 4 + h
                    mm(PS[:, bu * 512:bu * 512 + 128],
                       KTT[hp:hp + 64, (h * 4 + sub) * 128:(h * 4 + sub + 1) * 128],
                       VT[hp:hp + 64, sub, h * 128:(h + 1) * 128], True, True, ["KTT", "VT"], [("ps", bu)], True)
                for h in range(4):
                    o_out = PS[:, h * 512 + c * 64:h * 512 + (c + 1) * 64]
                    mm(o_out, SBF[:, h, :], QT[:, h, c * 64:(c + 1) * 64], False, True,
                       [("SBF", h), "QT"], [("ps", h)], True)
                for h in range(4):
                    bu = 4 + h
                    z = ZST[:, e, h, :]
                    stt(z, z, EBA[:, e, h, c:c + 1], PS[:, bu * 512:bu * 512 + 128],
                        ALU.mult, ALU.add, [(zk, h), "EBA%d" % e, ("ps", bu)], [(zk, h)])
                    act(SBF[:, h, :], z, AF.Identity, [(zk, h), "EBA%d" % e], [("SBF", h)], scale=EBA[:, e, h, c + 1:c + 2])
            if last_prompt:
                for h in range(4):
                    ts1(SFIN[:, h, :], ZST[:, e, h, :], EBA[:, e, h, 8:9], ALU.mult, [(zk, h), "EBA%d" % e], ["SFIN"])
                dst = o_hgrn_p[e].rearrange("h k v -> k h v")
                pr.dma("sp", lambda en: en.dma_start(out=dst, in_=SFIN[:, :, :]), "ohp", reads=["SFIN"], writes=[])
            act(EBA[:, e, :, 0], EBA[:, e, :, 8], AF.Copy, ["EBA%d" % e], ["EBA%d" % e])
            for h in range(4):
                rms_gate(e, h, N, bank(h), ("ps", h), SG[:, h, :], "SG", 6 + h % 2)
            out_proj_ln(l, ti, c0, N)

        def even_sample_tile(l, ti, c0):
            e = l // 2
            N = NB
            PZP = PS[:, 0:24 * N].rearrange("p (a b) -> p a b", a=24)
            for oc in list(range(0, 8)) + list(range(12, 24)):
                inproj_fm(oc, c0, N, ti, 0, col=oc * N)
            act(PZS[:, 0:8, :], PZP[:, 0:8, :], AF.Copy, [("ps", 0)], ["PZS"])
            act(PZS[:, 12:24, :], PZP[:, 12:24, :], AF.Copy, [("ps", 0)], ["PZS"])
            PZ = PZS
            mm_group(PS[0:N, 512:1024], [(XB[:, kc, c0:c0 + N], WIN[:, kc, 1024:1536]) for kc in range(8)],
                     [("WIN", 1), ("XB", ti)], ("ps", 1))
            small = lambda i: T[i][:, 0:4 * N].rearrange("p (a b) -> p a b", a=4)
            QS, FS, KK, SGS, TMPS, ACS = [small(i) for i in range(6)]
            VS = VSB[0:N, 0:512]
            VM = VMB[0:N, 0:512]
            act(QS, PZ[:, 0:4, :], AF.Sigmoid, ["PZS"], [("T", 0)])
            stt(QS, PZ[:, 0:4, :], QSCALE, QS, ALU.mult, ALU.mult, ["PZS", ("T", 0)], [("T", 0)])
            act(TMPS, PZ[:, 4:8, :], AF.Sigmoid, ["PZS"], [("T", 4)])
            for h in range(4):
                ts(FS[:, h, :], TMPS[:, h, :], LBV[:, 1, e, h:h + 1], LBV[:, 0, e, h:h + 1], ALU.mult, ALU.add,
                   [("T", 4), "LBV"], [("T", 1)])
            ts(KK, FS, -1.0, 1.0, ALU.mult, ALU.add, [("T", 1)], [("T", 2)])
            act(SGS, PZ[:, 12:16, :], AF.Sigmoid, ["PZS"], [("T", 3)])
            tt(SGS, PZ[:, 12:16, :], SGS, ALU.mult, ["PZS", ("T", 3)], [("T", 3)])
            act(VS, PS[0:N, 512:1024], AF.Copy, [("ps", 1)], ["VSB"])
            OUTUF = UREG[:, 0:4 * 16 * 30]
            OUTU = OUTUF.rearrange("p (a b c) -> p a b c", a=4, b=16)
            UN = T[4][:, 64:64 + 4 * N].rearrange("p (a b) -> p a b", a=4)
            srcu = s_conf[e].rearrange("(cc p) b k -> p cc (b k)", p=128)
            pr.dma("sp", lambda en: en.dma_start(out=UHF.rearrange("p (a b) -> p a b", a=4), in_=srcu), "ust",
                   writes=["UH"], bar=True)
            act(TMPS, PZ[:, 20:24, :], AF.Sigmoid, ["PZS"], [("T", 4)])
            tt(UN, PZ[:, 16:20, :], TMPS, ALU.mult, ["PZS", ("T", 4)], ["UN"])
            for cc in range(4):
                wap = PV[:, PV_DWW + (e * 4 + cc) * 31:PV_DWW + (e * 4 + cc) * 31 + 30]
                wbc = bass.AP(wap.tensor, wap.offset, [list(wap.ap[0]), [0, N], [1, 30]])
                w30 = PV[:, PV_DWW + (e * 4 + cc) * 31 + 30:PV_DWW + (e * 4 + cc) * 31 + 31]
                tt(TMPC, UH[:, cc, :, :], wbc, ALU.mult, ["UH", "PV"], ["TMPC"])
                pr.op("dve", lambda en, cc=cc: en.tensor_reduce(out=ACS[:, cc, :], in_=TMPC, axis=AX.X, op=ALU.add),
                      ["TMPC"], ["ACC"])
                stt(ACS[:, cc, :], UN[:, cc, :], w30, ACS[:, cc, :], ALU.mult, ALU.add, ["UN", "PV", "ACC"], ["ACC"])
                ts1(ACS[:, cc, :], ACS[:, cc, :], PV[:, PV_DWB + e * 4 + cc:PV_DWB + e * 4 + cc + 1], ALU.add,
                    ["ACC", "PV"], ["ACC"])
            act(OUTU[:, :, :, 0:29], UH[:, :, :, 1:30], AF.Copy, ["UH"], ["U"])
            act(OUTU[:, :, :, 29], UN, AF.Copy, ["UN"], ["U"])
            dstu = o_conf_s[e].rearrange("(cc p) b k -> p cc (b k)", p=128)
            pr.dma("sp", lambda en: en.dma_start(out=dstu, in_=OUTUF.rearrange("p (a b) -> p a b", a=4)), "ocs",
                   reads=["U"], bar=True)
            conf_ln_silu(e, N, ACS)
            PO = PS[:, 1024:1024 + 4 * N].rearrange("p (a b) -> p a b", a=4)
            VMS = [VMB[0:N, 0:512], MSQ[0:N, 0:512]]
            VMK = ["VMB", "MSQ"]

            def mk_vm(b):
                ts1(VMS[b % 2], VS, IDF[0:N, b:b + 1], ALU.mult, ["VSB", "CF"], [VMK[b % 2]])

            def mk_vbc(b):
                bx = 3 + (b % 2)
                mm(bank(bx), ONEF[0:N, :], VMS[b % 2], True, True, [VMK[b % 2], "CF"], [("ps", bx)], True)

            mk_vm(0)
            mk_vm(1)
            mk_vbc(0)
            for b in range(NB):
                q, bb = b // 4, b % 4
                sl = q % 2
                if bb == 0:
                    src = s_hgrn[e][:, 4 * q:4 * q + 4, :, :].rearrange("p q a b -> p (q a b)")
                    pr.dma("sp", lambda en, s=src, sl=sl: en.dma_start(out=SRF[sl], in_=s), "sr%d" % sl,
                           writes=[("SR", sl)], bar=True)
                if b + 1 < NB:
                    mk_vbc(b + 1)
                if b + 2 < NB:
                    mk_vm(b + 2)
                bx = 3 + (b % 2)
                for h in range(4):
                    ts1(KV[:, h, :], bank(bx)[:, h * 128:(h + 1) * 128], KK[:, h, b:b + 1], ALU.mult,
                        [("ps", bx), ("T", 2)], [("KV", h)])
                    stt(SR[sl][:, bb, h, :], SR[sl][:, bb, h, :], FS[:, h, b:b + 1], KV[:, h, :], ALU.mult, ALU.add,
                        [("SR", sl), ("T", 1), ("KV", h)], [("SR", sl)])
                for h in range(4):
                    mm(PO[:, h, b:b + 1], SR[sl][:, bb, h, :], QS[:, h, b:b + 1], True, True,
                       [("SR", sl), ("T", 0)], [("ps", 2)], True)
                if bb == 3:
                    dst = o_hgrn_s[e][:, 4 * q:4 * q + 4, :, :].rearrange("p q a b -> p (q a b)")
                    pr.dma("sp", lambda en, d=dst, sl=sl: en.dma_start(out=d, in_=SRF[sl]), "so%d" % sl,
                           reads=[("SR", sl)], bar=True)
            act(POS, PO, AF.Copy, [("ps", 2)], ["POS"])
            for h in range(4):
                rms_gate(e, h, N, POS[:, h, :], "POS", SGS[:, h, :], ("T", 3), 6 + h % 2)
            out_proj_ln(l, ti, c0, N)

        def odd_prompt_tile(l, ti, c0, last_prompt):
            o = l // 2
            N = 512
            act(ZB[:, :, 0:2], ZT[:, o, :, :], AF.Copy, ["ZT%d" % o], ["ZB"])
            for kc in range(8):
                ba, bc, bx = nb(6), nb(6), nb(6)
                inproj_fm(kc, c0, N, ti, ba)
                inproj_fm(8 + kc, c0, N, ti, bc)
                inproj_fm(16 + kc, c0, N, ti, bx)
                A = T[kc % 2][:, 0:N]
                Y = T[2 + kc % 2][:, 0:N]
                act(A, bank(bc, N), AF.Copy, [("ps", bc)], [("T", kc % 2)])
                z = ZB[:, kc, 2:2 + N]
                tt(z, bank(bx, N), A, ALU.mult, [("ps", bx), ("T", kc % 2)], [("ZB", kc)])
                w = lambda k: PV[:, PV_SCW + (o * 8 + kc) * 3 + k:PV_SCW + (o * 8 + kc) * 3 + k + 1]
                yk = ("T", 2 + kc % 2)
                ts1(Y, z, w(2), ALU.mult, [("ZB", kc), "PV"], [yk])
                stt(Y, ZB[:, kc, 1:1 + N], w(1), Y, ALU.mult, ALU.add, [("ZB", kc), "ZB", "PV", yk], [yk])
                stt(Y, ZB[:, kc, 0:N], w(0), Y, ALU.mult, ALU.add, [("ZB", kc), "ZB", "PV", yk], [yk])
                tt(M[:, kc, 0:N], bank(ba, N), Y, ALU.mult, [("ps", ba), yk], ["M"])
            act(ZT[:, o, :, :], ZB[:, :, N:N + 2], AF.Copy, ["ZB"] + [("ZB", kc) for kc in range(8)], ["ZT%d" % o])
            if last_prompt:
                dst = o_sconv_p[o].rearrange("(kc p) k -> p kc k", p=128)
                pr.dma("sp", lambda en: en.dma_start(out=dst, in_=ZT[:, o, :, :]), "osp", reads=["ZT%d" % o])
            out_proj_ln(l, ti, c0, N)

        def odd_sample_tile(l, ti, c0):
            o = l // 2
            N = NB
            PZP = PS[:, 0:24 * N].rearrange("p (a b) -> p a b", a=24)
            for oc in range(24):
                inproj_fm(oc, c0, N, ti, 0, col=oc * N)
            act(PZS, PZP, AF.Copy, [("ps", 0)], ["PZS"])
            PZ = PZS
            SS = T[0][:, 0:8 * N * 2].rearrange("p (a b c) -> p a b c", a=8, b=N)
            OS = T[1][:, 0:8 * N * 2].rearrange("p (a b c) -> p a b c", a=8, b=N)
            A = T[2][:, 0:8 * N].rearrange("p (a b) -> p a b", a=8)
            Zs = T[3][:, 0:8 * N].rearrange("p (a b) -> p a b", a=8)
            Y = T[4][:, 0:8 * N].rearrange("p (a b) -> p a b", a=8)
            src = s_sconv[o].rearrange("(kc p) b k -> p kc (b k)", p=128)
            pr.dma("sp", lambda en: en.dma_start(out=T[0][:, 0:8 * N * 2].rearrange("p (a b) -> p a b", a=8), in_=src),
                   "ssin", writes=[("T", 0)], bar=True)
            act(A, PZ[:, 8:16, :], AF.Copy, ["PZS"], [("T", 2)])
            tt(Zs, PZ[:, 16:24, :], A, ALU.mult, ["PZS", ("T", 2)], [("T", 3)])
            for kc in range(8):
                w = lambda k: PV[:, PV_SCW + (o * 8 + kc) * 3 + k:PV_SCW + (o * 8 + kc) * 3 + k + 1]
                ts1(Y[:, kc, :], Zs[:, kc, :], w(2), ALU.mult, [("T", 3), "PV"], [("T", 4)])
                stt(Y[:, kc, :], SS[:, kc, :, 1], w(1), Y[:, kc, :], ALU.mult, ALU.add, [("T", 0), "PV", ("T", 4)],
                    [("T", 4)])
                stt(Y[:, kc, :], SS[:, kc, :, 0], w(0), Y[:, kc, :], ALU.mult, ALU.add, [("T", 0), "PV", ("T", 4)],
                    [("T", 4)])
            tt(M[:, :, 0:N], PZ[:, 0:8, :], Y, ALU.mult, ["PZS", ("T", 4)], ["M"])
            act(OS[:, :, :, 0], SS[:, :, :, 1], AF.Copy, [("T", 0)], [("T", 1)])
            act(OS[:, :, :, 1], Zs, AF.Copy, [("T", 3)], [("T", 1)])
            dst = o_sconv_s[o].rearrange("(kc p) b k -> p kc (b k)", p=128)
            pr.dma("sp", lambda en: en.dma_start(out=dst, in_=T[1][:, 0:8 * N * 2].rearrange("p (a b) -> p a b", a=8)),
                   "ssout", reads=[("T", 1)], bar=True)
            out_proj_ln(l, ti, c0, N)

        def ffn_phase(l, tiles, next_win, is_last):
            pr.barrier()
            loads = []
            for g in range(2):
                for bi in range(3):
                    loads.append(("13", g, bi))
                for ob in range(4):
                    loads.append(("2", g, ob))
            issued = {"13": 0, "2": 0}
            seq = {"13": [x for x in loads if x[0] == "13"], "2": [x for x in loads if x[0] == "2"]}
            win_left = list(range(3)) if next_win is not None else []

            def issue(kind, idx):
                _, g, bi = seq[kind][idx]
                sl = idx % 2
                if kind == "13":
                    j0 = g * NJG + bi * 4
                    nj = min(4, g * NJG + NJG - j0)
                    for nm, wsrc, dstt in (("w1", ffn_w1, W1S), ("w3", ffn_w3, W3S)):
                        src = wsrc[l].rearrange("(kc p) n -> p kc n", p=128)[:, :, j0 * 128:(j0 + nj) * 128]
                        dst = dstt[sl][:, :, 0:nj * 128]
                        pr.dma("pool", lambda en, s=src, d=dst: en.dma_start(out=d, in_=s), "%s_%d" % (nm, sl),
                               writes=[(nm, sl)])
                else:
                    src = ffn_w2[l].rearrange("(j p) n -> p j n", p=128)[:, g * NJG:(g + 1) * NJG, bi * 256:(bi + 1) * 256]
                    dst = W2S[sl][:, :, :]
                    pr.dma("pool", lambda en, s=src, d=dst: en.dma_start(out=d, in_=s), "w2_%d" % sl,
                           writes=[("W2O", sl)])
                if win_left:
                    issue_win_piece(next_win, win_left.pop(0))

            def ensure(kind, idx):
                while issued[kind] <= min(idx + 1, len(seq[kind]) - 1):
                    issue(kind, issued[kind])
                    issued[kind] += 1

            ensure("13", 0)
            while pending_ln:
                pending_ln.pop(0)()
            i13 = 0
            i2 = 0
            pairs = [(0, 1), (2, 3), (4, 5), (6, 7)]
            pi = 0
            for g in range(2):
                for bi in range(3):
                    ensure("13", i13)
                    if bi == 1:
                        ensure("2", i2 - 1 if i2 > 0 else 0)
                    sl = i13 % 2
                    j0 = g * NJG + bi * 4
                    nj = min(4, g * NJG + NJG - j0)
                    for jj in range(nj):
                        jl = j0 + jj - g * NJG
                        for (ti, c0, N, dc, kind, gid) in tiles:
                            ba, bb = pairs[pi % 4]
                            pi += 1
                            mm_group(bank(ba, N), [(W1S[sl][:, kc, jj * 128:(jj + 1) * 128], XB[:, kc, c0:c0 + N])
                                                  for kc in range(8)], [("w1", sl), ("XB", ti)], ("ps", ba))
                            mm_group(bank(bb, N), [(W3S[sl][:, kc, jj * 128:(jj + 1) * 128], XB[:, kc, c0:c0 + N])
                                                  for kc in range(8)], [("w3", sl), ("XB", ti)], ("ps", bb))
                            ft = FT[pi % 2][:, 0:N]
                            act(ft, bank(ba, N), AF.Silu, [("ps", ba)], [("FT", pi % 2)])
                            tt(H[:, jl, c0:c0 + N], ft, bank(bb, N), ALU.mult, [("FT", pi % 2), ("ps", bb)],
                               [("H", ti)])
                    i13 += 1
                for ob in range(4):
                    ensure("2", i2)
                    sl = i2 % 2
                    for ol in range(2):
                        oc = ob * 2 + ol
                        for (ti, c0, N, dc, kind, gid) in tiles:
                            b = nb(6)
                            mm_group(bank(b, N), [(W2S[sl][:, jl, ol * 128:(ol + 1) * 128], H[:, jl, c0:c0 + N])
                                                 for jl in range(NJG)], [("W2O", sl), ("H", ti)], ("ps", b))
                            x = XF[:, oc, c0:c0 + N]
                            if g == 0:
                                stt(x, x, ALPHA, bank(b, N), ALU.mult, ALU.add, [("XF", ti, oc), ("ps", b)],
                                    [("XF", ti, oc)])
                            else:
                                tt(x, x, bank(b, N), ALU.add, [("XF", ti, oc), ("ps", b)], [("XF", ti, oc)])
                    i2 += 1
            while win_left:
                issue_win_piece(next_win, win_left.pop(0))
            pr.barrier()
            for (ti, c0, N, dc, kind, gid) in tiles:
                deepnorm_ln(ti, c0, N, PV_LNFW + l * 8, PV_LNFB + l * 8, dc if is_last else None)

        segs = [
            [(0, 0, 512, 0, "p", 0), (1, 512, 512, 512, "p", 1), (2, 1024, NB, 2048, "s", 4)],
            [(0, 0, 512, 1024, "p", 2), (1, 512, 512, 1536, "p", 3)],
        ]
        LAYERS = list(range(NL)) if isinstance(NL, int) else list(NL)
        import os as _os
        for _i in range(int(_os.environ.get("DUMMY_DVE", "0"))):
            pr.op("dve", lambda e: e.memset(BIASC[:, 3:4], 0.0), writes=["dummy"])
        for _i in range(int(_os.environ.get("DUMMY_ACT", "0"))):
            pr.op("act", lambda e: e.activation(out=BIASC[:, 2:3], in_=BIASC[:, 0:1], func=AF.Copy), writes=["dummy2"])
        for _i in range(int(_os.environ.get("DUMMY_SP", "0"))):
            pr.dma("sp", lambda e: e.dma_start(out=PV[:, 458:460], in_=pv_d[:, 458:460]), "dummysp", writes=["dummy3"])
        for _i in range(int(_os.environ.get("DUMMY_POOL", "0"))):
            pr.dma("pool", lambda e: e.dma_start(out=PV[:, 456:458], in_=pv_d[:, 456:458]), "dummypool", writes=["dummy4"])
        if _os.environ.get("ONESEG"):
            segs = segs[:1]
        if _os.environ.get("NOSAMPLE"):
            segs[0] = segs[0][:2]
        for i in ((2, 0, 1) if LAYERS[0] % 2 == 0 else (0, 1, 2)):
            issue_win_piece(LAYERS[0], i)
        for si, tiles in enumerate(segs):
            for (ti, c0, N, dc, kind, gid) in tiles:
                src = xT.rearrange("(kc p) t -> p kc t", p=128)[:, :, dc:dc + N]
                pr.dma("sp", lambda en, s=src, c0=c0, N=N: en.dma_start(out=XF[:, :, c0:c0 + N], in_=s), "x%d" % ti,
                       writes=[("XF", ti, oc) for oc in range(8)])
                act(XB[:, :, c0:c0 + N], XF[:, :, c0:c0 + N], AF.Copy, [("XF", ti, oc) for oc in range(8)], [("XB", ti)])
            for li, l in enumerate(LAYERS):
                issue_wout(l)
                for tix, (ti, c0, N, dc, kind, gid) in enumerate(tiles):
                    lastp = (gid == 3)
                    defer_flag[0] = (tix == len(tiles) - 1)
                    if kind == "s":
                        pr.barrier()
                    if l % 2 == 0:
                        if kind == "p":
                            even_prompt_tile(l, ti, c0, lastp)
                        else:
                            even_sample_tile(l, ti, c0)
                    else:
                        if kind == "p":
                            odd_prompt_tile(l, ti, c0, lastp)
                        else:
                            odd_sample_tile(l, ti, c0)
                    defer_flag[0] = False
                    if kind == "s" and tix != len(tiles) - 1:
                        pr.barrier()
                if li + 1 < len(LAYERS):
                    nxt = LAYERS[li + 1]
                elif si + 1 < len(segs):
                    nxt = LAYERS[0]
                else:
                    nxt = None
                ffn_phase(l, tiles, nxt, li == len(LAYERS) - 1)
        pr.finish()
        with nc.Block() as block:
            pr.emit(block)
    return nc


def _fm(a, nch):
    lead = a.shape[:-1]
    a = a.reshape(lead + (nch, 128))
    nd = a.ndim
    return np.ascontiguousarray(np.transpose(a, (nd - 1,) + tuple(range(nd - 2)) + (nd - 2,)))


def _pack_pv(inp):
    pv = np.zeros((128, NPV), np.float32)
    pv[:, PV_LNMW:PV_LNMW + 32] = _fm(inp["ln_mix_w"], 8).reshape(128, 32)
    pv[:, PV_LNMB:PV_LNMB + 32] = _fm(inp["ln_mix_b"], 8).reshape(128, 32)
    pv[:, PV_LNFW:PV_LNFW + 32] = _fm(inp["ln_ffn_w"], 8).reshape(128, 32)
    pv[:, PV_LNFB:PV_LNFB + 32] = _fm(inp["ln_ffn_b"], 8).reshape(128, 32)
    dw = _fm(inp["conf_dw_w"], 4)
    pv[:, PV_DWW:PV_DWW + 248] = np.transpose(dw, (0, 1, 3, 2)).reshape(128, 248)
    pv[:, PV_DWB:PV_DWB + 8] = _fm(inp["conf_dw_b"], 4).reshape(128, 8)
    pv[:, PV_CLW:PV_CLW + 8] = _fm(inp["conf_ln_w"], 4).reshape(128, 8)
    pv[:, PV_CLB:PV_CLB + 8] = _fm(inp["conf_ln_b"], 4).reshape(128, 8)
    scw = _fm(inp["sc_conv_w"], 8)
    pv[:, PV_SCW:PV_SCW + 48] = np.transpose(scw, (0, 1, 3, 2)).reshape(128, 48)
    pv[:, PV_LBL:PV_LBL + 8] = _fm(inp["hgrn_lb_logits"], 4).reshape(128, 8)
    pv[:, PV_GNW:PV_GNW + 2] = np.ascontiguousarray(inp["hgrn_gnorm_w"].T)
    return pv


def _consts():
    cf = np.zeros((128, NCST), np.float32)
    cb = np.zeros((128, NCST), np.float32)
    cf[:, C_ID:C_ID + 128] = np.eye(128, dtype=np.float32)
    cb[:, C_ID:C_ID + 128] = np.eye(128, dtype=np.float32)
    p = np.arange(128)[:, None] % 64
    t = np.arange(512)[None, :] % 64
    cb[:, C_TRI:C_TRI + 512] = (t >= p).astype(np.float32)
    cf[:, C_SM:C_SM + 512] = ((np.arange(512) % 64) != 0).astype(np.float32)[None, :]
    cf[:, C_ONE:C_ONE + 128] = 1.0
    cb[:, C_ONE:C_ONE + 128] = 1.0
    return cf, cb


_NC_CACHE = {}


def kernel(NL=DEPTH, **inp):
    inp = {k: np.asarray(v) for k, v in inp.items()}
    if NL not in _NC_CACHE:
        _NC_CACHE[NL] = build(NL)
    nc = _NC_CACHE[NL]
    pv = _pack_pv(inp)
    cstf, cstb = _consts()
    in_maps = []
    for c in range(NCORES):
        bs = slice(c * NB, (c + 1) * NB)
        xT = np.empty((D, NTOK), np.float32)
        xT[:, :2048] = inp["x_prompt"][c].T
        xT[:, 2048:] = inp["x_sample"][bs, 0, :].T
        in_maps.append({
            "xT": xT,
            "s_hgrn": np.ascontiguousarray(np.transpose(inp["state_hgrn"][:, bs], (0, 3, 1, 2, 4))),
            "s_conf": np.ascontiguousarray(np.transpose(inp["state_conf"][:, bs], (0, 3, 1, 2))),
            "s_sconv": np.ascontiguousarray(np.transpose(inp["state_sconv"][:, bs], (0, 3, 1, 2))),
            "w_in_even": inp["w_in_even"], "w_out_even": inp["w_out_even"],
            "sc_w_in": inp["sc_w_in"], "sc_w_out": inp["sc_w_out"],
            "ffn_w1": inp["ffn_w1"], "ffn_w3": inp["ffn_w3"], "ffn_w2": inp["ffn_w2"],
            "pv": pv, "cstf": cstf, "cstb": cstb,
        })
    res = run_bass_kernel_spmd(nc, in_maps, core_ids=list(range(NCORES)))
    R = res.results
    y_prompt = np.stack([R[c]["yT"][:, :2048].T for c in range(NCORES)])
    y_sample = np.concatenate([R[c]["yT"][:, 2048:].T for c in range(NCORES)])[:, None, :]
    h_p = np.stack([R[c]["o_hgrn_p"] for c in range(NCORES)], axis=1)
    c_p = np.stack([np.transpose(R[c]["o_conf_p"], (0, 2, 1)) for c in range(NCORES)], axis=1)
    s_p = np.stack([np.transpose(R[c]["o_sconv_p"], (0, 2, 1)) for c in range(NCORES)], axis=1)
    h_s = np.concatenate([np.transpose(R[c]["o_hgrn_s"], (0, 2, 3, 1, 4)) for c in range(NCORES)], axis=1)
    c_s = np.concatenate([np.transpose(R[c]["o_conf_s"], (0, 2, 3, 1)) for c in range(NCORES)], axis=1)
    s_s = np.concatenate([np.transpose(R[c]["o_sconv_s"], (0, 2, 3, 1)) for c in range(NCORES)], axis=1)
    f = lambda a: np.ascontiguousarray(a, dtype=np.float32)
    return (f(y_prompt), f(y_sample), f(h_p), f(c_p), f(s_p), f(h_s), f(c_s), f(s_s))
```
